# Optimizing a Trainium2 kernel written in Bass

```python
import math
import jax, jax.numpy as jnp
from jax import lax
import numpy as np

D_MODEL = 2048
BATCH = 4
SEQ = 4096
DEPTH = 4

N_MIXERS = 3
HEAD_DIM = 64
ROPE_DIM = HEAD_DIM // 4
ROPE_THETA = 500000.0
NORM_EPS = 1e-5

A_HEADS = D_MODEL // HEAD_DIM
A_KV_HEADS = A_HEADS // 8
A_GROUP = A_HEADS // A_KV_HEADS
A_WIDTH = A_HEADS * HEAD_DIM
A_KV_WIDTH = A_KV_HEADS * HEAD_DIM
A_IN = 2 * A_WIDTH + 2 * A_KV_WIDTH
A_WINDOW = 128
A_QBLOCK = 128

B_HEADS = D_MODEL // HEAD_DIM
B_KV_GROUPS = 4
B_GROUP = B_HEADS // B_KV_GROUPS
B_WIDTH = B_HEADS * HEAD_DIM
B_KV_WIDTH = B_KV_GROUPS * HEAD_DIM
B_N_BRANCH = 3
B_IN = 2 * B_WIDTH + 6 * B_KV_WIDTH + B_N_BRANCH * B_HEADS
CMP_BLOCK = 32
CMP_STRIDE = 16
CMP_HIDDEN = 4 * HEAD_DIM
SLC_BLOCK = 64
SLC_TOPK = 16
B_WINDOW = 512
B_QBLOCK = 64
WIN_QBLOCK = 128

C_WIDTH = D_MODEL
C_IN = 4 * C_WIDTH
CONV_WIDTH = 3

N_A = (DEPTH + 2) // N_MIXERS
N_B = (DEPTH + 1) // N_MIXERS
N_C = DEPTH // N_MIXERS

kernel_name = "hybrid_swa_nsa_shortconv_trunk"


def rmsnorm(x, g):
    xf = x.astype(jnp.float32)
    y = xf * lax.rsqrt(jnp.mean(xf * xf, axis=-1, keepdims=True) + NORM_EPS)
    return (y * g.astype(jnp.float32)).astype(x.dtype)


def rope_tables(positions):
    inv_freq = ROPE_THETA ** (-jnp.arange(0, ROPE_DIM, 2, dtype=jnp.float32) / ROPE_DIM)
    ang = positions.astype(jnp.float32)[..., None] * inv_freq
    return jnp.cos(ang)[:, :, None, :], jnp.sin(ang)[:, :, None, :]


def apply_rope(x, cos, sin):
    half = ROPE_DIM // 2
    x1 = x[..., :half].astype(jnp.float32)
    x2 = x[..., half:ROPE_DIM].astype(jnp.float32)
    r1 = x1 * cos - x2 * sin
    r2 = x2 * cos + x1 * sin
    return jnp.concatenate([r1.astype(x.dtype), r2.astype(x.dtype), x[..., ROPE_DIM:]], axis=-1)


def banded_attention(q, k, v, window, qblock, sinks=None):
    Bn, S, G, Hg, hd = q.shape
    nqb = S // qblock
    span = window + qblock
    scale = hd ** -0.5
    k_pad = jnp.pad(k, ((0, 0), (window, 0), (0, 0), (0, 0)))
    v_pad = jnp.pad(v, ((0, 0), (window, 0), (0, 0), (0, 0)))
    q_blocks = q.reshape(Bn, nqb, qblock, G, Hg, hd).swapaxes(0, 1)
    starts = jnp.arange(nqb, dtype=jnp.int32) * qblock
    rel = jnp.arange(qblock)[:, None] - jnp.arange(span)[None, :] + window
    band = (rel >= 0) & (rel < window)

    def step(args):
        qb, s0 = args
        kb = lax.dynamic_slice_in_dim(k_pad, s0, span, axis=1).astype(jnp.float32)
        vb = lax.dynamic_slice_in_dim(v_pad, s0, span, axis=1).astype(jnp.float32)
        s = jnp.einsum('bqghd,bkgd->bghqk', qb.astype(jnp.float32), kb) * scale
        valid = band & ((s0 - window + jnp.arange(span)) >= 0)[None, :]
        s = jnp.where(valid, s, -jnp.inf)
        if sinks is None:
            p = jax.nn.softmax(s, axis=-1)
        else:
            sk = sinks.astype(jnp.float32)[None, :, :, None, None]
            m = jnp.maximum(jnp.max(s, axis=-1, keepdims=True), sk)
            e = jnp.exp(s - m)
            p = e / (jnp.sum(e, axis=-1, keepdims=True) + jnp.exp(sk - m))
        o = jnp.einsum('bghqk,bkgd->bqghd', p, vb)
        return o.astype(q.dtype)

    o = lax.map(step, (q_blocks, starts))
    return o.swapaxes(0, 1).reshape(Bn, S, G, Hg, hd)


def mixer_a(h, w_in, sinks, w_out, cos, sin):
    Bn, S, _ = h.shape
    proj = h @ w_in
    q, k, v, z = jnp.split(proj, [A_WIDTH, A_WIDTH + A_KV_WIDTH, A_WIDTH + 2 * A_KV_WIDTH], axis=-1)
    q = apply_rope(q.reshape(Bn, S, A_HEADS, HEAD_DIM), cos, sin).reshape(Bn, S, A_KV_HEADS, A_GROUP, HEAD_DIM)
    k = apply_rope(k.reshape(Bn, S, A_KV_HEADS, HEAD_DIM), cos, sin)
    v = v.reshape(Bn, S, A_KV_HEADS, HEAD_DIM)
    o = banded_attention(q, k, v, A_WINDOW, A_QBLOCK, sinks.reshape(A_KV_HEADS, A_GROUP))
    o = o.reshape(Bn, S, A_WIDTH)
    return (o * jax.nn.silu(z)) @ w_out


def compress(k, pos, w1, w2):
    Bn, S, G, hd = k.shape
    nc = (S - CMP_BLOCK) // CMP_STRIDE + 1
    idx = np.arange(nc)[:, None] * CMP_STRIDE + np.arange(CMP_BLOCK)[None, :]
    blocks = k[:, idx] + pos[None, None, :, None, :]
    blocks = jnp.moveaxis(blocks, 3, 2).reshape(Bn, nc, G, CMP_BLOCK * hd)
    return jax.nn.gelu(blocks @ w1) @ w2


def nsa_cmp_slc(q, kcmp, vcmp, ks, vs):
    Bn, S, G, Hg, hd = q.shape
    nc = kcmp.shape[1]
    nsel = S // SLC_BLOCK
    n_top = min(SLC_TOPK, nsel)
    nqb = S // B_QBLOCK
    scale = hd ** -0.5
    c_start = np.arange(nc) * CMP_STRIDE
    c_end = c_start + CMP_BLOCK
    s_start = np.arange(nsel) * SLC_BLOCK
    s_end = s_start + SLC_BLOCK
    overlap = np.clip(np.minimum(c_end[:, None], s_end[None, :]) - np.maximum(c_start[:, None], s_start[None, :]), 0, None)
    overlap = jnp.asarray(overlap / CMP_BLOCK, jnp.float32)
    c_last = jnp.asarray(c_end - 1, jnp.int32)
    kcmp_f = kcmp.astype(jnp.float32)
    vcmp_f = vcmp.astype(jnp.float32)
    ks_blk = ks.reshape(Bn, nsel, SLC_BLOCK, G, hd).transpose(0, 1, 3, 2, 4)
    vs_blk = vs.reshape(Bn, nsel, SLC_BLOCK, G, hd).transpose(0, 1, 3, 2, 4)
    q_blocks = q.reshape(Bn, nqb, B_QBLOCK, G, Hg, hd).swapaxes(0, 1)
    starts = jnp.arange(nqb, dtype=jnp.int32) * B_QBLOCK
    bi = jnp.arange(Bn)[:, None, None, None]
    gi = jnp.arange(G)[None, None, :, None]
    blk = jnp.arange(nsel)

    def step(args):
        qb, s0 = args
        t = s0 + jnp.arange(B_QBLOCK)
        qf = qb.astype(jnp.float32)
        sc = jnp.einsum('bqghd,bcgd->bghqc', qf, kcmp_f) * scale
        sc = jnp.where(c_last[None, :] <= t[:, None], sc, -jnp.inf)
        m = jnp.max(sc, axis=-1, keepdims=True)
        m = jnp.where(jnp.isfinite(m), m, 0.0)
        e = jnp.exp(sc - m)
        p_cmp = e / jnp.maximum(jnp.sum(e, axis=-1, keepdims=True), 1e-30)
        o_cmp = jnp.einsum('bghqc,bcgd->bqghd', p_cmp, vcmp_f)
        imp = jnp.einsum('bghqc,cn->bgqn', p_cmp, overlap)
        cur = t // SLC_BLOCK
        forced = (blk[None, :] == 0) | (blk[None, :] == cur[:, None]) | (blk[None, :] == cur[:, None] - 1)
        future = blk[None, :] * SLC_BLOCK > t[:, None]
        imp = jnp.where(forced, jnp.inf, imp)
        imp = jnp.where(future, -jnp.inf, imp)
        _, sel = lax.top_k(imp, n_top)
        sel = sel.transpose(0, 2, 1, 3)
        kg = ks_blk[bi, sel, gi].astype(jnp.float32)
        vg = vs_blk[bi, sel, gi].astype(jnp.float32)
        ss = jnp.einsum('bqghd,bqgnld->bghqnl', qf, kg) * scale
        kpos = sel[..., None] * SLC_BLOCK + jnp.arange(SLC_BLOCK)
        valid = (kpos <= t[None, :, None, None, None]).transpose(0, 2, 1, 3, 4)[:, :, None]
        ss = jnp.where(valid, ss, -jnp.inf)
        shp = ss.shape
        p_slc = jax.nn.softmax(ss.reshape(shp[:4] + (n_top * SLC_BLOCK,)), axis=-1).reshape(shp)
        o_slc = jnp.einsum('bghqnl,bqgnld->bqghd', p_slc, vg)
        return o_cmp.astype(q.dtype), o_slc.astype(q.dtype)

    o_cmp, o_slc = lax.map(step, (q_blocks, starts))
    o_cmp = o_cmp.swapaxes(0, 1).reshape(Bn, S, G, Hg, hd)
    o_slc = o_slc.swapaxes(0, 1).reshape(Bn, S, G, Hg, hd)
    return o_cmp, o_slc


def mixer_b(h, w_in, kc_pos, kc_w1, kc_w2, vc_pos, vc_w1, vc_w2, w_out, cos, sin):
    Bn, S, _ = h.shape
    proj = h @ w_in
    cuts = [B_WIDTH + i * B_KV_WIDTH for i in range(7)] + [B_WIDTH + 6 * B_KV_WIDTH + B_N_BRANCH * B_HEADS]
    q, kc, vc, ks, vs, kw, vw, gates, z = jnp.split(proj, cuts, axis=-1)
    q = apply_rope(q.reshape(Bn, S, B_HEADS, HEAD_DIM), cos, sin).reshape(Bn, S, B_KV_GROUPS, B_GROUP, HEAD_DIM)
    kv_shape = (Bn, S, B_KV_GROUPS, HEAD_DIM)
    ks = apply_rope(ks.reshape(kv_shape), cos, sin)
    kw = apply_rope(kw.reshape(kv_shape), cos, sin)
    kcmp = compress(kc.reshape(kv_shape), kc_pos, kc_w1, kc_w2)
    vcmp = compress(vc.reshape(kv_shape), vc_pos, vc_w1, vc_w2)
    o_cmp, o_slc = nsa_cmp_slc(q, kcmp, vcmp, ks, vs.reshape(kv_shape))
    o_win = banded_attention(q, kw, vw.reshape(kv_shape), B_WINDOW, WIN_QBLOCK)
    g = jax.nn.sigmoid(gates.astype(jnp.float32)).reshape(Bn, S, B_N_BRANCH, B_KV_GROUPS, B_GROUP, 1)
    o = (g[:, :, 0] * o_cmp.astype(jnp.float32) + g[:, :, 1] * o_slc.astype(jnp.float32)
         + g[:, :, 2] * o_win.astype(jnp.float32)).astype(h.dtype)
    o = o.reshape(Bn, S, B_WIDTH)
    return (o * jax.nn.silu(z)) @ w_out


def mixer_c(h, w_in, conv_w, w_out):
    u, b_gate, c_gate, z = jnp.split(h @ w_in, 4, axis=-1)
    v = c_gate * u
    y = lax.conv_general_dilated(v, conv_w[:, None, :], window_strides=(1,), padding=[(CONV_WIDTH - 1, 0)],
                                 dimension_numbers=('NWC', 'WIO', 'NWC'), feature_group_count=C_WIDTH)
    y = b_gate * y
    return (y * jax.nn.silu(z)) @ w_out


def setup_inputs(seed: int = 0) -> dict:
    key = jax.random.key(seed)
    ks = jax.random.split(key, 20)

    def dense(k, shape, fan_in):
        return jax.random.normal(k, shape, jnp.float32) * fan_in ** -0.5

    x = jax.random.normal(ks[0], (BATCH, SEQ, D_MODEL), jnp.float32)
    positions = (jnp.arange(SEQ, dtype=jnp.int32)[None, :]
                 + jax.random.randint(ks[1], (BATCH, 1), 0, 1024, dtype=jnp.int32))
    norm_w = 1.0 + 0.02 * jax.random.normal(ks[2], (DEPTH, D_MODEL), jnp.float32)
    final_norm_w = 1.0 + 0.02 * jax.random.normal(ks[3], (D_MODEL,), jnp.float32)
    a_w_in = dense(ks[4], (N_A, D_MODEL, A_IN), D_MODEL)
    a_sinks = jax.random.normal(ks[5], (N_A, A_HEADS), jnp.float32)
    a_w_out = dense(ks[6], (N_A, A_WIDTH, D_MODEL), A_WIDTH)
    b_w_in = dense(ks[7], (N_B, D_MODEL, B_IN), D_MODEL)
    b_cmp_k_pos = 0.1 * jax.random.normal(ks[8], (N_B, CMP_BLOCK, HEAD_DIM), jnp.float32)
    b_cmp_k_w1 = dense(ks[9], (N_B, CMP_BLOCK * HEAD_DIM, CMP_HIDDEN), CMP_BLOCK * HEAD_DIM)
    b_cmp_k_w2 = dense(ks[10], (N_B, CMP_HIDDEN, HEAD_DIM), CMP_HIDDEN)
    b_cmp_v_pos = 0.1 * jax.random.normal(ks[11], (N_B, CMP_BLOCK, HEAD_DIM), jnp.float32)
    b_cmp_v_w1 = dense(ks[12], (N_B, CMP_BLOCK * HEAD_DIM, CMP_HIDDEN), CMP_BLOCK * HEAD_DIM)
    b_cmp_v_w2 = dense(ks[13], (N_B, CMP_HIDDEN, HEAD_DIM), CMP_HIDDEN)
    b_w_out = dense(ks[14], (N_B, B_WIDTH, D_MODEL), B_WIDTH)
    c_w_in = dense(ks[15], (N_C, D_MODEL, C_IN), D_MODEL)
    c_conv_w = dense(ks[16], (N_C, CONV_WIDTH, C_WIDTH), CONV_WIDTH)
    c_w_out = dense(ks[17], (N_C, C_WIDTH, D_MODEL), C_WIDTH)
    return {"x": x, "positions": positions, "norm_w": norm_w, "final_norm_w": final_norm_w,
            "a_w_in": a_w_in, "a_sinks": a_sinks, "a_w_out": a_w_out,
            "b_w_in": b_w_in, "b_cmp_k_pos": b_cmp_k_pos, "b_cmp_k_w1": b_cmp_k_w1, "b_cmp_k_w2": b_cmp_k_w2,
            "b_cmp_v_pos": b_cmp_v_pos, "b_cmp_v_w1": b_cmp_v_w1, "b_cmp_v_w2": b_cmp_v_w2, "b_w_out": b_w_out,
            "c_w_in": c_w_in, "c_conv_w": c_conv_w, "c_w_out": c_w_out}


def reference(x, positions, norm_w, final_norm_w, a_w_in, a_sinks, a_w_out,
              b_w_in, b_cmp_k_pos, b_cmp_k_w1, b_cmp_k_w2, b_cmp_v_pos, b_cmp_v_w1, b_cmp_v_w2, b_w_out,
              c_w_in, c_conv_w, c_w_out):
    cos, sin = rope_tables(positions)
    for i in range(DEPTH):
        h = rmsnorm(x, norm_w[i])
        kind, j = i % N_MIXERS, i // N_MIXERS
        if kind == 0:
            out = mixer_a(h, a_w_in[j], a_sinks[j], a_w_out[j], cos, sin)
        elif kind == 1:
            out = mixer_b(h, b_w_in[j], b_cmp_k_pos[j], b_cmp_k_w1[j], b_cmp_k_w2[j],
                          b_cmp_v_pos[j], b_cmp_v_w1[j], b_cmp_v_w2[j], b_w_out[j], cos, sin)
        else:
            out = mixer_c(h, c_w_in[j], c_conv_w[j], c_w_out[j])
        x = x + out.astype(x.dtype)
    return rmsnorm(x, final_norm_w)
```

```python
import contextlib
import numpy as np
import concourse.bass as bass
import concourse.mybir as mybir
from concourse.bass_utils import run_bass_kernel_spmd

F32 = mybir.dt.float32
BF16 = mybir.dt.bfloat16
I32 = mybir.dt.int32
ALU = mybir.AluOpType
AF = mybir.ActivationFunctionType
AX = mybir.AxisListType

SEQ = 4096
D = 2048
NT = SEQ // 128
KC = D // 128
HD = 64
EPS = 1e-5
A_IN = 4608
B_IN = 5728
C_IN = 8192
SC_T = 2048
SC_TILES = SC_T // 128
PI = float(np.pi)
BIG = 1.0e30


class Res:
    __slots__ = ("name", "last_w", "readers")

    def __init__(self, name=""):
        self.name = name
        self.last_w = None
        self.readers = []


class Op:
    __slots__ = ("eng", "fn", "deps", "needs_signal", "sig", "dma_sem", "dma_need")

    def __init__(self, eng, fn, dma_sem):
        self.eng = eng
        self.fn = fn
        self.deps = set()
        self.needs_signal = False
        self.sig = None
        self.dma_sem = dma_sem


class Sched:
    ENGS = ("tensor", "vector", "scalar", "gpsimd", "sync")

    def __init__(self, nc):
        self.nc = nc
        self.ops = []
        self.dma_sems = []
        self.dma_count = {}
        self.bar_deps = set()
        self.bar_dma = {}

    def new_dma_sem(self, name):
        if name not in self.dma_count:
            self.dma_sems.append(name)
            self.dma_count[name] = 0
        return name

    def barrier(self):
        last = {}
        for o in self.ops:
            if o.dma_sem is None:
                last[o.eng] = o
        self.bar_deps = set(last.values())
        self.bar_dma = {k: v for k, v in self.dma_count.items() if v}

    def op(self, eng, fn, reads=(), writes=(), dma_sem=None):
        o = Op(eng, fn, dma_sem)
        for r in reads:
            if r.last_w is not None:
                o.deps.add(r.last_w)
        for w in writes:
            if w.last_w is not None:
                o.deps.add(w.last_w)
            for rd in w.readers:
                o.deps.add(rd)
        for r in reads:
            r.readers.append(o)
        for w in writes:
            w.last_w = o
            w.readers = []
        o.deps.discard(o)
        if eng == "tensor":
            o.deps = {d for d in o.deps if not (d.eng == "tensor" and d.dma_sem is None)}
        o.dma_need = dict(self.bar_dma)
        o.deps |= {d for d in self.bar_deps if not (eng == "tensor" and d.eng == "tensor")}
        for d in o.deps:
            if d.dma_sem is not None:
                o.dma_need[d.dma_sem] = self.dma_count[d.dma_sem]
            else:
                d.needs_signal = True
        if dma_sem is not None:
            self.dma_count[dma_sem] += 16
        self.ops.append(o)
        return o

    def emit(self):
        nc = self.nc
        with contextlib.ExitStack() as st:
            esem = {e: st.enter_context(nc.semaphore("e_" + e)) for e in self.ENGS}
            dsem = {n: st.enter_context(nc.semaphore("d_" + n)) for n in self.dma_sems}
            ecount = {e: 0 for e in self.ENGS}
            dcount = {n: 0 for n in self.dma_sems}
            for o in self.ops:
                if o.dma_sem is not None:
                    dcount[o.dma_sem] += 16
                    o.sig = (("d", o.dma_sem), dcount[o.dma_sem])
                elif o.needs_signal:
                    ecount[o.eng] += 1
                    o.sig = (("e", o.eng), ecount[o.eng])
            per_eng = {e: [] for e in self.ENGS}
            for o in self.ops:
                per_eng[o.eng].append(o)
            block = st.enter_context(nc.Block())

            def make(ename):
                def body(eng):
                    waited = {}
                    for o in per_eng[ename]:
                        need = {}
                        for d in o.deps:
                            if d.dma_sem is not None:
                                continue
                            k, v = d.sig
                            if v > need.get(k, 0):
                                need[k] = v
                        for n, v in o.dma_need.items():
                            need[("d", n)] = v
                        for k, v in need.items():
                            if waited.get(k, 0) >= v:
                                continue
                            waited[k] = v
                            s = esem[k[1]] if k[0] == "e" else dsem[k[1]]
                            eng.wait_ge(s, v)
                        ins = o.fn(eng)
                        if o.sig is not None:
                            k, v = o.sig
                            if k[0] == "d":
                                ins.then_inc(dsem[k[1]], 16)
                            else:
                                ins.then_inc(esem[k[1]], 1)
                    if ename == "sync":
                        for n, c in dcount.items():
                            if c:
                                eng.wait_ge(dsem[n], c)
                        for e, c in ecount.items():
                            if c:
                                eng.wait_ge(esem[e], c)
                return body

            block.tensor(make("tensor"))
            block.vector(make("vector"))
            block.scalar(make("scalar"))
            block.gpsimd(make("gpsimd"))
            block.sync(make("sync"))
        self.stats = dict(ecount=ecount, dcount=dcount, nops=len(self.ops))


class T:
    __slots__ = ("ap", "res")

    def __init__(self, ap, name=""):
        self.ap = ap
        self.res = Res(name)


class Ring:
    def __init__(self, items, sems=None):
        self.items = items
        self.sems = sems
        self.i = -1

    def next(self):
        self.i = (self.i + 1) % len(self.items)
        if self.sems is None:
            return self.items[self.i]
        return self.items[self.i], self.sems[self.i]


class Builder:
    def __init__(self, layers=(0, 1, 2, 3), final=True, dbg=False):
        self.layers = layers
        self.final = final
        self.dbg = dbg
        self.nc = bass.Bass("TRN2", target_bir_lowering=False)
        self.S = Sched(self.nc)
        self.st = contextlib.ExitStack()
        self.alt = 0

    def sb(self, name, shape, dt):
        return T(self.st.enter_context(self.nc.sbuf_tensor(name, shape, dt)), name)

    def dram_in(self, name, shape, dt):
        return self.nc.dram_tensor(name, shape, dt, kind="ExternalInput").ap()

    def ring_sb(self, name, n, shape, dt, sem=True):
        items = [self.sb(f"{name}{i}", shape, dt) for i in range(n)]
        sems = [self.S.new_dma_sem(f"{name}{i}") for i in range(n)] if sem else None
        return Ring(items, sems)

    def evac_eng(self):
        self.alt ^= 1
        return "scalar" if self.alt else "vector"

    def copy(self, eng, out_ap, in_ap, reads, writes):
        if eng == "scalar":
            self.S.op("scalar", lambda e: e.activation(out=out_ap, in_=in_ap, func=AF.Copy), reads, writes)
        else:
            self.S.op(eng, lambda e: e.tensor_copy(out=out_ap, in_=in_ap), reads, writes)

    def declare(self):
        nc, S = self.nc, self.S
        di = self.dram_in
        self.x_in = di("x", [SEQ, D], F32)
        self.posT = di("posT", [128, NT], I32)
        self.norm_w = di("norm_w", [4, D], F32)
        self.final_norm_w = di("final_norm_w", [1, D], F32)
        self.a_w_in = di("a_w_in", [2, D, A_IN], F32)
        self.a_sinks = di("a_sinks", [2, 32], F32)
        self.a_w_out = di("a_w_out", [2, D, D], F32)
        self.b_w_in = di("b_w_in", [1, D, B_IN], F32)
        self.b_k_pos = di("b_cmp_k_pos", [1, 32, 64], F32)
        self.b_k_w1 = di("b_cmp_k_w1", [1, 2048, 256], F32)
        self.b_k_w2 = di("b_cmp_k_w2", [1, 256, 64], F32)
        self.b_v_pos = di("b_cmp_v_pos", [1, 32, 64], F32)
        self.b_v_w1 = di("b_cmp_v_w1", [1, 2048, 256], F32)
        self.b_v_w2 = di("b_cmp_v_w2", [1, 256, 64], F32)
        self.b_w_out = di("b_w_out", [1, D, D], F32)
        self.c_w_in = di("c_w_in", [1, D, C_IN], F32)
        self.c_conv_w = di("c_conv_w", [1, 3, D], F32)
        self.c_w_out = di("c_w_out", [1, D, D], F32)
        self.consts = di("consts", [128, 128], F32)
        self.ovl = di("ovl", [2, 128, 64], F32)
        self.out = nc.dram_tensor("out", [SEQ, D], F32, kind="ExternalOutput").ap()
        self.xcur = nc.dram_tensor("xcur", [SEQ, D], F32, kind="ExternalOutput" if self.dbg else "Internal").ap()
        self.proj = nc.dram_tensor("proj", [SEQ, C_IN], BF16, kind="ExternalInput" if self.dbg == "attn" else "Internal").ap()
        self.ybuf = nc.dram_tensor("ybuf", [SEQ, D], BF16, kind="ExternalOutput" if self.dbg == "attn" else "Internal").ap()
        self.xin_res = [Res(f"xin{i}") for i in range(NT)]
        self.xcur_res = [Res(f"xcur{i}") for i in range(NT)]
        self.proj_res = [Res(f"proj{i}") for i in range(NT)]
        self.y_res = [Res(f"y{i}") for i in range(NT)]
        self.out_res = [Res(f"out{i}") for i in range(NT)]

        sb = self.sb
        self.ident = sb("ident", [128, 128], BF16)
        self.identf = sb("identf", [128, 128], F32)
        self.cst = sb("cst", [128, 128], F32)
        self.cos = sb("cos", [128, NT, 8], F32)
        self.sin = sb("sin", [128, NT, 8], F32)
        self.epsb = sb("epsb", [128, 1], F32)
        self.ARENA_BYTES = 188 * 1024
        self.arena = sb("arena", [128, self.ARENA_BYTES // 2], BF16)
        self.bank = []
        for i in range(8):
            t = T(self.st.enter_context(nc.psum_tensor(f"bank{i}", [128, 512], F32)), f"bank{i}")
            self.bank.append(t)
        self.acc_ring = Ring([self.bank[0], self.bank[1]])
        self.tr_ring = Ring([self.bank[2], self.bank[3]])

    def stage(self):
        self.S.barrier()
        self.arena_off = 0

    def sems(self, prefix, n):
        return [self.S.new_dma_sem(f"{prefix}{k}") for k in range(n)]

    def gemm_alloc(self):
        self.stage()
        self.tr_ring = Ring([self.bank[2], self.bank[3]])
        cv = self.carve
        self.gw = cv("gw", [128, D], F32)
        self.AT = [cv(f"AT{i}", [128, KC, 128], BF16) for i in range(SC_TILES)]
        self.W = self.carve_ring("W", 2, [128, KC, 512], BF16, self.sems("gW", 2))
        self.XIN = self.carve_ring("XIN", 2, [128, D], F32, self.sems("gX", 2))
        self.YIN = self.carve_ring("YIN", 2, [128, D], BF16, self.sems("gY", 2))
        self.HB = self.carve_ring("HB", 2, [128, D], BF16)
        self.OSF = self.carve_ring("OSF", 2, [128, 512], F32, self.sems("gOF", 2))
        self.OSB = self.carve_ring("OSB", 2, [128, 512], BF16, self.sems("gOB", 2))
        self.XR = self.carve_ring("XR", 2, [128, 512], F32, self.sems("gXR", 2))
        self.ss = self.carve_ring("ss", 4, [128, 2], F32)
        self.FO = self.carve_ring("FO", 2, [128, D], F32, self.sems("gFO", 2))

    def setup(self):
        S = self
        sch = self.S
        cs = sch.new_dma_sem("setup")
        cst, ident, identf = self.cst, self.ident, self.identf
        sch.op("sync", lambda e: e.dma_start(out=cst.ap[:], in_=self.consts[:, :]), writes=[cst.res], dma_sem=cs)
        sch.op("gpsimd", lambda e: e.memset(identf.ap[:], 1.0), writes=[identf.res])
        sch.op("gpsimd", lambda e: e.affine_select(out=identf.ap[:], in_=identf.ap[:], pattern=[[-1, 128]],
                                                   compare_op=ALU.is_equal, fill=0.0, base=0, channel_multiplier=1),
               reads=[identf.res], writes=[identf.res])
        sch.op("vector", lambda e: e.tensor_copy(out=ident.ap[:], in_=identf.ap[:]), reads=[identf.res], writes=[ident.res])
        sch.op("vector", lambda e: e.memset(self.epsb.ap[:], EPS), writes=[self.epsb.res])
        posi = self.sb("posi", [128, NT], I32)
        posf = self.sb("posf", [128, NT], F32)
        ang = self.sb("ang", [128, NT, 8], F32)
        tmp = self.sb("angt", [128, NT, 8], F32)
        ki = self.sb("angk", [128, NT, 8], I32)
        kf = self.sb("angkf", [128, NT, 8], F32)
        sch.op("sync", lambda e: e.dma_start(out=posi.ap[:], in_=self.posT[:, :]), writes=[posi.res], dma_sem=cs)
        sch.op("vector", lambda e: e.tensor_copy(out=posf.ap[:], in_=posi.ap[:]), reads=[posi.res], writes=[posf.res])
        sch.op("vector", lambda e: e.tensor_tensor(out=ang.ap[:], in0=posf.ap[:].unsqueeze(2).to_broadcast([128, NT, 8]),
                                                   in1=cst.ap[:, 0:8].unsqueeze(1).to_broadcast([128, NT, 8]), op=ALU.mult),
               reads=[posf.res, cst.res], writes=[ang.res])
        for which, shift in ((self.sin, 0.0), (self.cos, PI / 2)):
            sch.op("vector", lambda e, shift=shift: e.tensor_scalar(out=tmp.ap[:], in0=ang.ap[:], scalar1=shift, scalar2=None, op0=ALU.add),
                   reads=[ang.res], writes=[tmp.res])
            sch.op("vector", lambda e: e.tensor_scalar(out=ki.ap[:], in0=tmp.ap[:], scalar1=1.0 / (2 * PI), scalar2=None, op0=ALU.mult),
                   reads=[tmp.res], writes=[ki.res])
            sch.op("vector", lambda e: e.tensor_copy(out=kf.ap[:], in_=ki.ap[:]), reads=[ki.res], writes=[kf.res])
            sch.op("vector", lambda e: e.scalar_tensor_tensor(out=tmp.ap[:], in0=kf.ap[:], scalar=-2 * PI, in1=tmp.ap[:],
                                                              op0=ALU.mult, op1=ALU.add), reads=[kf.res, tmp.res], writes=[tmp.res])
            sch.op("vector", lambda e: e.tensor_scalar(out=tmp.ap[:], in0=tmp.ap[:], scalar1=3.14159, scalar2=-3.14159,
                                                       op0=ALU.min, op1=ALU.max), reads=[tmp.res], writes=[tmp.res])
            sch.op("scalar", lambda e, which=which: e.activation(out=which.ap[:], in_=tmp.ap[:], func=AF.Sin),
                   reads=[tmp.res], writes=[which.res])

    def transposes(self, src_ap, src_res, nblk, dst_fn, dst_res):
        sch = self.S
        ident = self.ident
        c = 0
        while c < nblk:
            n = min(4, nblk - c)
            bk = self.tr_ring.next()
            pv = bk.ap[:].bitcast(BF16)
            for j in range(n):
                sch.op("tensor", lambda e, j=j, c=c, pv=pv: e.transpose(out=pv[:, j * 128:(j + 1) * 128],
                                                                      in_=src_ap[:, (c + j) * 128:(c + j + 1) * 128],
                                                                      identity=ident.ap[:]),
                       reads=[src_res, ident.res], writes=[bk.res])
            dst = dst_fn(c, n)
            srcv = pv[:, 0:n * 128].rearrange("p (n t) -> p n t", t=128)
            self.copy(self.evac_eng(), dst, srcv, [bk.res], dst_res)
            c += n

    def gemm(self, kind, w_ap, ncols, layer, src_x=None, src_x_res=None):
        sch = self.S
        nc = self.nc
        self.gemm_alloc()
        wv = w_ap.rearrange("(kc p) n -> p kc n", p=128)
        ncb = (ncols + 511) // 512
        if kind == "in":
            gsem = sch.new_dma_sem(f"gw{layer}")
            gw = self.gw
            sch.op("sync", lambda e: e.dma_start(out=gw.ap[:], in_=self.norm_w[layer:layer + 1, :].to_broadcast([128, D])),
                   writes=[gw.res], dma_sem=gsem)
        for sc in range(NT // SC_TILES):
            for tl in range(SC_TILES):
                i = sc * SC_TILES + tl
                at = self.AT[tl]
                if kind == "in":
                    xt, xsem = self.XIN.next()
                    sch.op("sync", lambda e, xt=xt, i=i: e.dma_start(out=xt.ap[:], in_=src_x[i * 128:(i + 1) * 128, :]),
                           reads=[src_x_res[i]], writes=[xt.res], dma_sem=xsem)
                    ss = self.ss.next()
                    hb = self.HB.next()
                    sch.op("scalar", lambda e, xt=xt, ss=ss, hb=hb: e.activation(out=hb.ap[:], in_=xt.ap[:], func=AF.Square,
                                                                        accum_out=ss.ap[:, 0:1]),
                           reads=[xt.res], writes=[hb.res, ss.res])
                    sch.op("scalar", lambda e, ss=ss: e.activation(out=ss.ap[:, 1:2], in_=ss.ap[:, 0:1], func=AF.Sqrt,
                                                                 bias=self.epsb.ap[:, 0:1], scale=1.0 / D),
                           reads=[ss.res, self.epsb.res], writes=[ss.res])
                    sch.op("vector", lambda e, ss=ss: e.reciprocal(out=ss.ap[:, 1:2], in_=ss.ap[:, 1:2]),
                           reads=[ss.res], writes=[ss.res])
                    sch.op("vector", lambda e, xt=xt, ss=ss, hb=hb: e.scalar_tensor_tensor(
                        out=hb.ap[:], in0=xt.ap[:], scalar=ss.ap[:, 1:2], in1=self.gw.ap[:], op0=ALU.mult, op1=ALU.mult),
                        reads=[xt.res, ss.res, self.gw.res], writes=[hb.res])
                    a_ap, a_res = hb.ap, hb.res
                else:
                    yt, ysem = self.YIN.next()
                    sch.op("sync", lambda e, yt=yt, i=i: e.dma_start(out=yt.ap[:], in_=self.ybuf[i * 128:(i + 1) * 128, :]),
                           reads=[self.y_res[i]], writes=[yt.res], dma_sem=ysem)
                    a_ap, a_res = yt.ap, yt.res
                self.transposes(a_ap, a_res, KC, lambda c, n, at=at: at.ap[:, c:c + n, :], [at.res])
            for cb in range(ncb):
                c0 = cb * 512
                cw = min(512, ncols - c0)
                wt, wsem = self.W.next()
                sch.op("gpsimd", lambda e, wt=wt, c0=c0, cw=cw: e.dma_start(out=wt.ap[:, :, 0:cw], in_=wv[:, :, c0:c0 + cw]),
                       writes=[wt.res], dma_sem=wsem)
                for tl in range(SC_TILES):
                    i = sc * SC_TILES + tl
                    at = self.AT[tl]
                    acc = self.acc_ring.next()
                    for kc in range(KC):
                        sch.op("tensor", lambda e, acc=acc, at=at, wt=wt, kc=kc, cw=cw: e.matmul(
                            acc.ap[:, 0:cw], lhsT=at.ap[:, kc, :], rhs=wt.ap[:, kc, 0:cw], start=(kc == 0), stop=(kc == KC - 1)),
                            reads=[at.res, wt.res], writes=[acc.res])
                    if kind == "in":
                        ob, osem = self.OSB.next()
                        self.copy(self.evac_eng(), ob.ap[:, 0:cw], acc.ap[:, 0:cw], [acc.res], [ob.res])
                        sch.op("sync", lambda e, ob=ob, i=i, c0=c0, cw=cw: e.dma_start(
                            out=self.proj[i * 128:(i + 1) * 128, c0:c0 + cw], in_=ob.ap[:, 0:cw]),
                            reads=[ob.res], writes=[self.proj_res[i]], dma_sem=osem)
                    else:
                        xr, xsem = self.XR.next()
                        sch.op("sync", lambda e, xr=xr, i=i, c0=c0, cw=cw: e.dma_start(
                            out=xr.ap[:, 0:cw], in_=src_x[i * 128:(i + 1) * 128, c0:c0 + cw]),
                            reads=[src_x_res[i]], writes=[xr.res], dma_sem=xsem)
                        of, osem = self.OSF.next()
                        sch.op("vector", lambda e, of=of, acc=acc, xr=xr, cw=cw: e.tensor_tensor(
                            out=of.ap[:, 0:cw], in0=acc.ap[:, 0:cw], in1=xr.ap[:, 0:cw], op=ALU.add),
                            reads=[acc.res, xr.res], writes=[of.res])
                        sch.op("sync", lambda e, of=of, i=i, c0=c0, cw=cw: e.dma_start(
                            out=self.xcur[i * 128:(i + 1) * 128, c0:c0 + cw], in_=of.ap[:, 0:cw]),
                            reads=[of.res], writes=[self.xcur_res[i]], dma_sem=osem)

    def final_norm(self):
        sch = self.S
        self.gemm_alloc()
        gsem = sch.new_dma_sem("gwf")
        gw = self.gw
        sch.op("sync", lambda e: e.dma_start(out=gw.ap[:], in_=self.final_norm_w[0:1, :].to_broadcast([128, D])),
               writes=[gw.res], dma_sem=gsem)
        fo = self.FO
        for i in range(NT):
            xt, xsem = self.XIN.next()
            sch.op("sync", lambda e, xt=xt, i=i: e.dma_start(out=xt.ap[:], in_=self.xcur[i * 128:(i + 1) * 128, :]),
                   reads=[self.xcur_res[i]], writes=[xt.res], dma_sem=xsem)
            ss = self.ss.next()
            ft, fsem = fo.next()
            sch.op("scalar", lambda e, xt=xt, ss=ss, ft=ft: e.activation(out=ft.ap[:], in_=xt.ap[:], func=AF.Square,
                                                                accum_out=ss.ap[:, 0:1]),
                   reads=[xt.res], writes=[ft.res, ss.res])
            sch.op("scalar", lambda e, ss=ss: e.activation(out=ss.ap[:, 1:2], in_=ss.ap[:, 0:1], func=AF.Sqrt,
                                                         bias=self.epsb.ap[:, 0:1], scale=1.0 / D),
                   reads=[ss.res, self.epsb.res], writes=[ss.res])
            sch.op("vector", lambda e, ss=ss: e.reciprocal(out=ss.ap[:, 1:2], in_=ss.ap[:, 1:2]), reads=[ss.res], writes=[ss.res])
            sch.op("vector", lambda e, xt=xt, ss=ss, ft=ft: e.scalar_tensor_tensor(
                out=ft.ap[:], in0=xt.ap[:], scalar=ss.ap[:, 1:2], in1=gw.ap[:], op0=ALU.mult, op1=ALU.mult),
                reads=[xt.res, ss.res, gw.res], writes=[ft.res])
            sch.op("sync", lambda e, ft=ft, i=i: e.dma_start(out=self.out[i * 128:(i + 1) * 128, :], in_=ft.ap[:]),
                   reads=[ft.res], writes=[self.out_res[i]], dma_sem=fsem)


    def carve(self, name, shape, dt, parts=128):
        esz = 4 if dt in (F32, I32) else 2
        n = int(np.prod(shape[1:]))
        nbytes = (n * esz + 31) // 32 * 32
        o2 = self.arena_off // 2
        ap = self.arena.ap[0:parts, o2:o2 + nbytes // 2]
        if esz == 4:
            ap = ap.bitcast(dt)
        ap = ap[:, 0:n]
        if len(shape) == 3:
            ap = ap.rearrange("p (a b) -> p a b", b=shape[2])
        elif len(shape) == 4:
            ap = ap.rearrange("p (a b c) -> p a b c", b=shape[2], c=shape[3])
        self.arena_off += nbytes
        assert self.arena_off <= self.ARENA_BYTES, (name, self.arena_off)
        return T(ap, name)

    def carve_ring(self, name, n, shape, dt, sems=None):
        return Ring([self.carve(f"{name}{i}", shape, dt) for i in range(n)], sems)

    def conv_layer(self, j):
        sch = self.S
        self.stage()
        CW = 1024
        NB = 3
        lsem = [sch.new_dma_sem(f"cvl{k}") for k in range(NB)]
        ssem = [sch.new_dma_sem(f"cvs{k}") for k in range(NB)]
        wsem = sch.new_dma_sem("cvw")
        U = self.carve_ring("cvU", NB, [128, 3, CW], BF16, lsem)
        Cg = self.carve_ring("cvC", NB, [128, 3, CW], BF16, lsem)
        Bg = self.carve_ring("cvB", NB, [128, CW], BF16, lsem)
        Zg = self.carve_ring("cvZ", NB, [128, CW], BF16, lsem)
        V = self.carve_ring("cvV", 2, [128, 3, CW], F32)
        SZ = self.carve_ring("cvSZ", 2, [128, CW], F32)
        YO = self.carve_ring("cvY", NB, [128, CW], BF16, ssem)
        CWT = self.carve("cvW", [128, 3, CW], F32)
        for cc in range(D // CW):
            c0 = cc * CW
            sch.op("sync", lambda e, c0=c0: e.dma_start(
                out=CWT.ap[:], in_=self.c_conv_w[j:j + 1, :, c0:c0 + CW].to_broadcast([128, 3, CW])),
                writes=[CWT.res], dma_sem=wsem)
            for i in range(NT):
                u, ls = U.next()
                c, _ = Cg.next()
                b, _ = Bg.next()
                z, _ = Zg.next()
                v = V.next()
                sz = SZ.next()
                yo, ss_ = YO.next()
                r0 = i * 128
                if i == 0:
                    for s in range(2):
                        sh = 2 - s
                        sch.op("gpsimd", lambda e, u=u, s=s, sh=sh: e.memset(u.ap[0:sh, s, :], 0.0), writes=[u.res])
                        sch.op("gpsimd", lambda e, c=c, s=s, sh=sh: e.memset(c.ap[0:sh, s, :], 0.0), writes=[c.res])
                    for s in range(3):
                        sh = 2 - s
                        sch.op("sync", lambda e, u=u, s=s, sh=sh, c0=c0: e.dma_start(
                            out=u.ap[sh:128, s, :], in_=self.proj[0:128 - sh, c0:c0 + CW]),
                            reads=[self.proj_res[0]], writes=[u.res], dma_sem=ls)
                        sch.op("sync", lambda e, c=c, s=s, sh=sh, c0=c0: e.dma_start(
                            out=c.ap[sh:128, s, :], in_=self.proj[0:128 - sh, 2 * D + c0:2 * D + c0 + CW]),
                            reads=[self.proj_res[0]], writes=[c.res], dma_sem=ls)
                else:
                    for (dst, cb) in ((u, c0), (c, 2 * D + c0)):
                        base = self.proj[r0 - 2:r0 + 126, cb:cb + CW]
                        src = bass.AP(tensor=base.tensor, offset=base.offset,
                                      ap=[list(base.ap[0]), [base.ap[0][0], 3], list(base.ap[1])])
                        sch.op("sync", lambda e, dst=dst, src=src: e.dma_start(out=dst.ap[:], in_=src),
                               reads=[self.proj_res[i], self.proj_res[i - 1]], writes=[dst.res], dma_sem=ls)
                sch.op("scalar", lambda e, b=b, r0=r0, c0=c0: e.dma_start(
                    out=b.ap[:], in_=self.proj[r0:r0 + 128, D + c0:D + c0 + CW]),
                    reads=[self.proj_res[i]], writes=[b.res], dma_sem=ls)
                sch.op("scalar", lambda e, z=z, r0=r0, c0=c0: e.dma_start(
                    out=z.ap[:], in_=self.proj[r0:r0 + 128, 3 * D + c0:3 * D + c0 + CW]),
                    reads=[self.proj_res[i]], writes=[z.res], dma_sem=ls)
                sch.op("vector", lambda e, u=u, c=c, v=v: e.tensor_tensor(out=v.ap[:], in0=u.ap[:], in1=c.ap[:], op=ALU.mult),
                       reads=[u.res, c.res], writes=[v.res])
                sch.op("gpsimd", lambda e, v=v: e.tensor_tensor(out=v.ap[:], in0=v.ap[:], in1=CWT.ap[:], op=ALU.mult),
                       reads=[v.res, CWT.res], writes=[v.res])
                sch.op("vector", lambda e, v=v: e.tensor_tensor(out=v.ap[:, 0, :], in0=v.ap[:, 0, :], in1=v.ap[:, 1, :], op=ALU.add),
                       reads=[v.res], writes=[v.res])
                sch.op("vector", lambda e, v=v: e.tensor_tensor(out=v.ap[:, 0, :], in0=v.ap[:, 0, :], in1=v.ap[:, 2, :], op=ALU.add),
                       reads=[v.res], writes=[v.res])
                sch.op("vector", lambda e, v=v, b=b: e.tensor_tensor(out=v.ap[:, 0, :], in0=v.ap[:, 0, :], in1=b.ap[:], op=ALU.mult),
                       reads=[v.res, b.res], writes=[v.res])
                sch.op("scalar", lambda e, z=z, sz=sz: e.activation(out=sz.ap[:], in_=z.ap[:], func=AF.Silu),
                       reads=[z.res], writes=[sz.res])
                sch.op("vector", lambda e, v=v, sz=sz, yo=yo: e.tensor_tensor(out=yo.ap[:], in0=v.ap[:, 0, :], in1=sz.ap[:], op=ALU.mult),
                       reads=[v.res, sz.res], writes=[yo.res])
                sch.op("sync", lambda e, yo=yo, r0=r0, c0=c0: e.dma_start(out=self.ybuf[r0:r0 + 128, c0:c0 + CW], in_=yo.ap[:]),
                       reads=[yo.res], writes=[self.y_res[i]], dma_sem=ss_)

    def rope(self, eng, x3, res, nh, i):
        sch = self.S
        tmp = self.ropet.next()
        cb = self.cos.ap[:, i, :].unsqueeze(1).to_broadcast([128, nh, 8])
        sb_ = self.sin.ap[:, i, :].unsqueeze(1).to_broadcast([128, nh, 8])
        x1 = x3[:, :, 0:8]
        x2 = x3[:, :, 8:16]
        t = tmp.ap[:, :, 0:nh, :]
        rr = [res, tmp.res, self.cos.res, self.sin.res]
        sch.op(eng, lambda e: e.tensor_tensor(out=t[:, 0], in0=x1, in1=cb, op=ALU.mult), rr, [tmp.res])
        sch.op(eng, lambda e: e.tensor_tensor(out=t[:, 1], in0=x2, in1=sb_, op=ALU.mult), rr, [tmp.res])
        sch.op(eng, lambda e: e.tensor_tensor(out=t[:, 2], in0=x2, in1=cb, op=ALU.mult), rr, [tmp.res])
        sch.op(eng, lambda e: e.tensor_tensor(out=t[:, 3], in0=x1, in1=sb_, op=ALU.mult), rr, [tmp.res])
        sch.op(eng, lambda e: e.tensor_tensor(out=x1, in0=t[:, 0], in1=t[:, 1], op=ALU.subtract), [tmp.res], [res])
        sch.op(eng, lambda e: e.tensor_tensor(out=x2, in0=t[:, 2], in1=t[:, 3], op=ALU.add), [tmp.res], [res])

    def attn_alloc(self, mode):
        sch = self.S
        self.stage()
        cv = self.carve
        self.KT = [cv("KT0", [128, SEQ], BF16), cv("KT1", [128, SEQ], BF16)]
        self.V1 = [cv("V10", [128, NT, 72], BF16), cv("V11", [128, NT, 72], BF16)]
        if not hasattr(self, "att_sems"):
            self.att_sems = {k: [sch.new_dma_sem(f"at_{k}{n}") for n in range(3)] for k in ("kl", "zl", "gl")}
            for k in ("ql", "yo"):
                self.att_sems[k] = [sch.new_dma_sem(f"at_{k}{n}") for n in range(4)]
            self.att_misc = sch.new_dma_sem("at_misc")
        self.KL = self.carve_ring("KL", 3, [128, 6, 64], BF16, self.att_sems["kl"])
        self.KD = self.carve_ring("KD", 3, [128, 3, 2, 64], BF16)
        self.ropet = self.carve_ring("ropet", 4, [128, 4, 8, 8], F32)
        self.QL = self.carve_ring("QL", 4, [128, 8, 64], BF16, self.att_sems["ql"])
        self.ZL = self.carve_ring("ZL", 3, [128, 512], BF16, self.att_sems["zl"])
        self.GL = self.carve_ring("GL", 3, [128, 3, 8], BF16, self.att_sems["gl"])
        self.QT = self.carve_ring("QT", 4, [128, 4, 2, 128], BF16)
        self.PT = self.carve_ring("PT", 8, [128, 512], BF16)
        self.SZ = self.carve_ring("SZ", 3, [128, 512], F32)
        self.YO = self.carve_ring("YO", 4, [128, 512], BF16, self.att_sems["yo"])
        self.small = self.carve_ring("small", 6, [128, 64], F32)
        self.SZall = cv("SZall", [128, NT, 512], BF16)
        self.SGall = cv("SGall", [128, NT, 24], F32)
        self.esink = cv("esink", [128, 32], F32)
        if mode == "nsa":
            self.KVC = cv("KVC", [128, SEQ], BF16)
            self.W1 = [cv("W1k", [128, 32, 256], BF16), cv("W1v", [128, 32, 256], BF16)]
            self.POST = cv("POST", [128, 32], BF16)
            self.HBIAS = cv("HBIAS", [128, 4], F32)
            self.W2K = cv("W2K", [128, 2, 2, 64], BF16)
            self.W2V = cv("W2V", [128, 2, 64], BF16)
            self.HID = [cv("HIDk", [128, 2, 256], BF16), cv("HIDv", [128, 2, 256], BF16)]
            self.GU = self.carve_ring("GU", 2, [128, 3, 256], F32)
            self.KCT = cv("KCT", [128, 256], BF16)
            self.VC1 = cv("VC1", [128, 2, 136], BF16)
            self.EXPD = cv("EXPD", [64, 32, 2, 64], BF16, parts=64)
            self.OC = self.carve_ring("OC", 2, [128, 8, 64], F32)
            self.TMPC = self.carve_ring("TMPC", 2, [128, 4, 64], F32)
            self.IMP = self.carve_ring("IMP", 2, [128, 3, 64], F32)
            self.M8 = self.carve_ring("M8", 2, [128, 16], F32)
            self.SEL = self.carve_ring("SEL", 2, [128, 128], BF16)
            self.SELT = self.carve_ring("SELT", 3, [128, 128], BF16)
            self.MK = self.carve_ring("MK", 2, [128, NT, 128], BF16)
        if mode == "swa":
            self.TMPF = self.carve_ring("TMPF", 4, [128, 4, 64], F32)
        self.s_ring = Ring([self.bank[4], self.bank[5]])
        self.s_ring3 = Ring([self.bank[4], self.bank[5], self.bank[0]])
        self.tr_ring = Ring([self.bank[2]]) if mode == "swa" else Ring([self.bank[2], self.bank[3]])
        self.s_ring4 = Ring([self.bank[4], self.bank[5], self.bank[0], self.bank[1]])
        self.o_slc = self.bank[6]
        self.o_win = self.bank[7]
        self.o_cmp = Ring([self.bank[0], self.bank[1]])

    def attn_layer(self, mode, j):
        sch = self.S
        self.attn_alloc(mode)
        ms = self.att_misc
        for qt in self.QT.items:
            sch.op("gpsimd", lambda e, qt=qt: e.memset(qt.ap[:], 0.0), writes=[qt.res])
        for k in range(2):
            v1 = self.V1[k]
            sch.op("gpsimd", lambda e, v1=v1: e.memset(v1.ap[:, :, 64:65], 1.0), writes=[v1.res])
        if mode == "swa":
            es = self.esink
            sch.op("sync", lambda e: e.dma_start(out=es.ap[:], in_=self.a_sinks[j:j + 1, :].to_broadcast([128, 32])),
                   writes=[es.res], dma_sem=ms)
            sch.op("scalar", lambda e: e.activation(out=es.ap[:], in_=es.ap[:], func=AF.Exp), [es.res], [es.res])
            zbase = 2048 + 512
        else:
            if "nosetup" not in getattr(self, "skip", ()):
                self.nsa_setup(j)
            zbase = 3680
        for g in getattr(self, "test_g", range(4)):
            if "nokprep" not in getattr(self, "skip", ()):
                self.kprep(mode, g)
            if mode == "nsa" and "nocompress" not in getattr(self, "skip", ()):
                self.compress(g)
            tiles = list(getattr(self, "test_i", range(NT)))
            for i in tiles:
                zl, zs = self.ZL.next()
                zsrc = self.proj[i * 128:(i + 1) * 128, zbase + g * 512:zbase + (g + 1) * 512]
                sch.op("sync", lambda e, zl=zl, zsrc=zsrc: e.dma_start(out=zl.ap[:], in_=zsrc),
                       reads=[self.proj_res[i]], writes=[zl.res], dma_sem=zs)
                sch.op("scalar", lambda e, zl=zl, i=i: e.activation(out=self.SZall.ap[:, i, :], in_=zl.ap[:], func=AF.Silu),
                       [zl.res], [self.SZall.res])
            if mode == "nsa":
                for i in tiles:
                    gl, gs = self.GL.next()
                    gsrc = self.proj[i * 128:(i + 1) * 128, 3584:3680].rearrange("p (b h) -> p b h", h=32)[:, :, g * 8:(g + 1) * 8]
                    sch.op("sync", lambda e, gl=gl, gsrc=gsrc: e.dma_start(out=gl.ap[:], in_=gsrc),
                           reads=[self.proj_res[i]], writes=[gl.res], dma_sem=gs)
                    sch.op("scalar", lambda e, gl=gl, i=i: e.activation(
                        out=self.SGall.ap[:, i, :], in_=gl.ap[:].rearrange("p b h -> p (b h)"), func=AF.Sigmoid),
                        [gl.res], [self.SGall.res])
            if mode == "swa":
                pairs = [tiles[k:k + 2] for k in range(0, len(tiles), 2)]
                ctxs = [self.q_prep(mode, g, i, zbase) for i in pairs[0]] if pairs else []
                pend = []
                for n, pr in enumerate(pairs):
                    nxt = [self.q_prep(mode, g, i, zbase) for i in pairs[n + 1]] if n + 1 < len(pairs) else []
                    for p_ in pend:
                        p_()
                    pend = self.swa_pair(g, pr, ctxs)
                    ctxs = nxt
                for p_ in pend:
                    p_()
            else:
                ctx = self.q_prep(mode, g, tiles[0], zbase) if tiles else None
                pend = None
                for n, i in enumerate(tiles):
                    nxt = self.q_prep(mode, g, tiles[n + 1], zbase) if n + 1 < len(tiles) else None
                    if pend is not None:
                        pend()
                    pend = self.q_core(mode, g, i, ctx)
                    ctx = nxt
                if pend is not None:
                    pend()

    def swa_pair(self, g, tiles, ctxs):
        sch = self.S
        obanks = [self.bank[6], self.bank[7], self.bank[1], self.bank[3]]
        per = []
        for ti, (i, ctx) in enumerate(zip(tiles, ctxs)):
            kts = list(range(max(0, i - 1), i + 1))
            lst = []
            for n, kt in enumerate(kts):
                for half in range(2):
                    lst.append(dict(KT=self.KT[0], V1=self.V1[0], ob=obanks[ti * 2 + half], half=half, kt=kt, first=(n == 0),
                                    last=(n == len(kts) - 1), causal=(kt == i), band=(kt == i - 1), mk=None, qt=ctx["qt"]))
            per.append(lst)
        tasks = []
        for k in range(max(len(l) for l in per)):
            for l in per:
                if k < len(l):
                    tasks.append(l[k])
        self.run_tasks(None, tasks, L=2, ring=self.s_ring3)
        stores = []
        for ti, (i, ctx) in enumerate(zip(tiles, ctxs)):
            sz = ctx["sz"]
            yo, ys = self.YO.next()
            r0 = i * 128
            for half in range(2):
                ob = obanks[ti * 2 + half]
                sm = self.small.next()
                ov = ob.ap[:, 0:272].rearrange("p (h c) -> p h c", c=68)
                es = self.esink
                h0 = g * 8 + half * 4
                sch.op("vector", lambda e, sm=sm, ov=ov, h0=h0: e.tensor_tensor(
                    out=sm.ap[:, 0:4], in0=ov[:, :, 64], in1=es.ap[:, h0:h0 + 4], op=ALU.add), [ob.res, es.res], [sm.res])
                sch.op("vector", lambda e, sm=sm: e.reciprocal(out=sm.ap[:, 0:4], in_=sm.ap[:, 0:4]), [sm.res], [sm.res])
                c0 = half * 256
                tmpf = self.TMPF.next()
                sch.op("vector", lambda e, sm=sm, ov=ov, tmpf=tmpf: e.tensor_tensor(
                    out=tmpf.ap[:], in0=ov[:, :, 0:64], in1=sm.ap[:, 0:4].unsqueeze(2).to_broadcast([128, 4, 64]), op=ALU.mult),
                    [ob.res, sm.res], [tmpf.res])
                sch.op("vector", lambda e, tmpf=tmpf, c0=c0, yo=yo, sz=sz: e.tensor_tensor(
                    out=yo.ap[:, c0:c0 + 256], in0=tmpf.ap[:].rearrange("p h d -> p (h d)"), in1=sz.ap[:, c0:c0 + 256], op=ALU.mult),
                    [tmpf.res, sz.res], [yo.res])

            def store(yo=yo, ys=ys, r0=r0, i=i):
                sch.op("sync", lambda e: e.dma_start(out=self.ybuf[r0:r0 + 128, g * 512:(g + 1) * 512], in_=yo.ap[:]),
                       reads=[yo.res], writes=[self.y_res[i]], dma_sem=ys)
            stores.append(store)
        return stores

    def kprep(self, mode, g):
        sch = self.S
        npc = 2 if mode == "swa" else 6
        SK = getattr(self, "skip", ())
        for i in range(getattr(self, "test_nk", NT)):
            kl, ks = self.KL.next()
            kd = self.KD.next()
            r0 = i * 128
            src = self.proj[r0:r0 + 128, 2048:2048 + npc * 256].rearrange("p (m c) -> p m c", c=256)[:, :, g * 64:(g + 1) * 64]
            sch.op("sync", lambda e, kl=kl, src=src: e.dma_start(out=kl.ap[:, 0:npc, :], in_=src),
                   reads=[self.proj_res[i]], writes=[kl.res], dma_sem=ks)
            if mode == "swa":
                ropes, dups, vs = [0], [(0, 0)], [(1, 0)]
            else:
                ropes, dups, vs = [2, 4], [(2, 0), (4, 1)], [(3, 0), (5, 1)]
            for m in ropes:
                if "k_norope" not in SK:
                    self.rope("gpsimd", kl.ap[:, m:m + 1, :], kl.res, 1, i)
            for m, slot in dups:
                sch.op("vector", lambda e, kl=kl, kd=kd, m=m, slot=slot: e.tensor_copy(
                    out=kd.ap[:, slot], in_=kl.ap[:, m:m + 1, :].to_broadcast([128, 2, 64])), [kl.res], [kd.res])
            if mode == "nsa":
                sch.op("vector", lambda e, kl=kl, kd=kd: e.tensor_copy(out=kd.ap[:, 2], in_=kl.ap[:, 0:2, :]), [kl.res], [kd.res])
            for m, k in (vs if "k_nov" not in SK else []):
                v1 = self.V1[k]
                sch.op("gpsimd", lambda e, kl=kl, v1=v1, m=m, i=i: e.tensor_copy(out=v1.ap[:, i, 0:64], in_=kl.ap[:, m, :]),
                       [kl.res], [v1.res])
            nd = 1 if mode == "swa" else 3
            dsts = [self.KT[0]] if mode == "swa" else [self.KT[0], self.KT[1], self.KVC]
            bk = self.tr_ring.next()
            pv = bk.ap[:].bitcast(BF16)
            kdf = kd.ap[:].rearrange("p a b c -> p (a b c)")
            for n in range(nd):
                sch.op("tensor", lambda e, n=n, pv=pv, kdf=kdf: e.transpose(
                    out=pv[:, n * 128:(n + 1) * 128], in_=kdf[:, n * 128:(n + 1) * 128], identity=self.ident.ap[:]),
                    reads=[kd.res, self.ident.res], writes=[bk.res])
            ev = self.evac_eng()
            for n in range(nd):
                dst = dsts[n]
                self.copy(ev, dst.ap[:, r0:r0 + 128], pv[:, n * 128:(n + 1) * 128], [bk.res], [dst.res])

    def q_prep(self, mode, g, i, zbase):
        sch = self.S
        r0 = i * 128
        ql, qs = self.QL.next()
        sch.op("sync", lambda e: e.dma_start(out=ql.ap[:].rearrange("p h d -> p (h d)"),
                                             in_=self.proj[r0:r0 + 128, g * 512:(g + 1) * 512]),
               reads=[self.proj_res[i]], writes=[ql.res], dma_sem=qs)
        self.rope("gpsimd" if (mode == "nsa" or i % 2 == 0) else "vector", ql.ap[:], ql.res, 8, i)
        qt = self.QT.next()
        bkq = self.tr_ring.next()
        pvq = bkq.ap[:].bitcast(BF16)
        qlf = ql.ap[:].rearrange("p h d -> p (h d)")
        for c in range(4):
            sch.op("tensor", lambda e, c=c: e.transpose(out=pvq[:, c * 128:(c + 1) * 128], in_=qlf[:, c * 128:(c + 1) * 128],
                                                       identity=self.ident.ap[:]), [ql.res, self.ident.res], [bkq.res])
        pq3 = pvq[:, 0:512].rearrange("p (n t) -> p n t", t=128)
        self.copy("vector", qt.ap[0:64, :, 0, :], pq3[0:64], [bkq.res], [qt.res])
        self.copy("scalar", qt.ap[64:128, :, 1, :], pq3[64:128], [bkq.res], [qt.res])
        sz = T(self.SZall.ap[:, i, :]); sz.res = self.SZall.res
        sg = None
        if mode == "nsa":
            sg = T(self.SGall.ap[:, i, :]); sg.res = self.SGall.res
        return dict(qt=qt, sz=sz, gl=sg)

    def q_core(self, mode, g, i, ctx):
        sch = self.S
        r0 = i * 128
        qt, sz, gl = ctx["qt"], ctx["sz"], ctx["gl"]
        yo, ys = self.YO.next()
        if mode == "swa":
            obs = [self.o_slc, self.o_win]
            kts = list(range(max(0, i - 1), i + 1))
            tasks = []
            for n, kt in enumerate(kts):
                for half in range(2):
                    tasks.append(dict(KT=self.KT[0], V1=self.V1[0], ob=obs[half], half=half, kt=kt, first=(n == 0),
                                      last=(n == len(kts) - 1), causal=(kt == i), band=(kt == i - 1), mk=None))
            self.run_tasks(qt, tasks)
            for half in range(2):
                ob = obs[half]
                sm = self.small.next()
                ov = ob.ap[:, 0:272].rearrange("p (h c) -> p h c", c=68)
                es = self.esink
                h0 = g * 8 + half * 4
                sch.op("vector", lambda e, sm=sm, ov=ov, h0=h0: e.tensor_tensor(
                    out=sm.ap[:, 0:4], in0=ov[:, :, 64], in1=es.ap[:, h0:h0 + 4], op=ALU.add), [ob.res, es.res], [sm.res])
                sch.op("vector", lambda e, sm=sm: e.reciprocal(out=sm.ap[:, 0:4], in_=sm.ap[:, 0:4]), [sm.res], [sm.res])
                for hl in range(4):
                    c0 = (half * 4 + hl) * 64
                    sch.op("vector", lambda e, sm=sm, ov=ov, hl=hl, c0=c0: e.scalar_tensor_tensor(
                        out=yo.ap[:, c0:c0 + 64], in0=ov[:, hl, 0:64], scalar=sm.ap[:, hl:hl + 1], in1=sz.ap[:, c0:c0 + 64],
                        op0=ALU.mult, op1=ALU.mult), [ob.res, sm.res, sz.res], [yo.res])
        else:
            self.nsa_q(g, i, qt, gl, sz, yo)
        def store():
            sch.op("sync", lambda e: e.dma_start(out=self.ybuf[r0:r0 + 128, g * 512:(g + 1) * 512], in_=yo.ap[:]),
                   reads=[yo.res], writes=[self.y_res[i]], dma_sem=ys)
        return store

    def run_tasks(self, qt, tasks, L=3, ring=None):
        sch = self.S
        banks = {}

        def issue_qk(idx):
            t = tasks[idx]
            bk = (ring or self.s_ring4).next()
            banks[idx] = bk
            kt = t["kt"]
            self.qk(bk, t.get("qt", qt), t["half"], t["KT"].ap[:, kt * 128:(kt + 1) * 128], t["KT"].res)

        for idx in range(min(L, len(tasks))):
            issue_qk(idx)
        for idx, t in enumerate(tasks):
            if idx + L < len(tasks):
                issue_qk(idx + L)
            cur = banks.pop(idx)
            pt = self.PT.next()
            sch.op("scalar", lambda e, pt=pt, cur=cur: e.activation(out=pt.ap[:], in_=cur.ap[:], func=AF.Exp, scale=0.125),
                   [cur.res], [pt.res])
            p3 = pt.ap[:].rearrange("p (h t) -> p h t", t=128)
            if t["mk"] is not None:
                mk, kt = t["mk"], t["kt"]
                eng = "gpsimd" if idx % 8 == 7 else "vector"
                sch.op(eng, lambda e, p3=p3, mk=mk, kt=kt: e.tensor_tensor(
                    out=p3, in0=p3, in1=mk.ap[:, kt:kt + 1, :].to_broadcast([128, 4, 128]), op=ALU.mult),
                    [pt.res, mk.res], [pt.res])
            if t["causal"]:
                sch.op("gpsimd", lambda e, p3=p3: e.affine_select(out=p3, in_=p3, pattern=[[0, 4], [1, 128]],
                                                                  compare_op=ALU.is_ge, fill=0.0, base=0, channel_multiplier=-1),
                       [pt.res], [pt.res])
            if t["band"]:
                sch.op("gpsimd", lambda e, p3=p3: e.affine_select(out=p3, in_=p3, pattern=[[0, 4], [-1, 128]],
                                                                  compare_op=ALU.is_ge, fill=0.0, base=-1, channel_multiplier=1),
                       [pt.res], [pt.res])
            ob, V1, kt = t["ob"], t["V1"], t["kt"]
            for hl in range(4):
                sch.op("tensor", lambda e, hl=hl, pt=pt, kt=kt, ob=ob, V1=V1, first=t["first"], last=t["last"]: e.matmul(
                    ob.ap[:, hl * 68:hl * 68 + 65], lhsT=pt.ap[:, hl * 128:(hl + 1) * 128], rhs=V1.ap[:, kt, 0:65],
                    start=(first and hl == 0), stop=last, skip_group_check=True),
                    reads=[pt.res, V1.res], writes=[ob.res])

    def qk(self, sbk, qt, half, kt_ap, kres, width=128, bias=None):
        sch = self.S
        for hl in range(4):
            hh = half * 4 + hl
            pr, od = hh // 2, hh % 2
            sch.op("tensor", lambda e, hl=hl, pr=pr, od=od: e.matmul(
                sbk.ap[0:width, hl * 128:(hl + 1) * 128], lhsT=kt_ap, rhs=qt.ap[:, pr, od, :],
                start=(hl == 0), stop=(bias is None), skip_group_check=True), reads=[kres, qt.res], writes=[sbk.res])
        if bias is not None:
            bl, br, bres = bias
            sch.op("tensor", lambda e: e.matmul(
                sbk.ap[:, 0:512], lhsT=bl, rhs=br.unsqueeze(1).to_broadcast([64, 4, 128]),
                start=False, stop=True, skip_group_check=True), reads=list(bres), writes=[sbk.res])

    def banded(self, qt, half, i, wt, KT, V1, ob, after_exp=None):
        sch = self.S
        kts = list(range(max(0, i - wt), i + 1))
        sb0 = self.s_ring.next()
        self.qk(sb0, qt, half, KT.ap[:, kts[0] * 128:(kts[0] + 1) * 128], KT.res)
        cur = sb0
        for n, kt in enumerate(kts):
            nxt = None
            if n + 1 < len(kts):
                nxt = self.s_ring.next()
                k2 = kts[n + 1]
                self.qk(nxt, qt, half, KT.ap[:, k2 * 128:(k2 + 1) * 128], KT.res)
            pt = self.PT.next()
            sch.op("scalar", lambda e, pt=pt, cur=cur: e.activation(out=pt.ap[:], in_=cur.ap[:], func=AF.Exp, scale=0.125),
                   [cur.res], [pt.res])
            p3 = pt.ap[:].rearrange("p (h t) -> p h t", t=128)
            SK = getattr(self, "skip", ())
            if kt == i and "mask" not in SK:
                sch.op("gpsimd", lambda e, p3=p3: e.affine_select(out=p3, in_=p3, pattern=[[0, 4], [1, 128]],
                                                                  compare_op=ALU.is_ge, fill=0.0, base=0, channel_multiplier=-1),
                       [pt.res], [pt.res])
            if kt == i - wt and "mask" not in SK:
                sch.op("gpsimd", lambda e, p3=p3: e.affine_select(out=p3, in_=p3, pattern=[[0, 4], [-1, 128]],
                                                                  compare_op=ALU.is_ge, fill=0.0, base=-1, channel_multiplier=1),
                       [pt.res], [pt.res])
            for hl in range(4):
                sch.op("tensor", lambda e, hl=hl, pt=pt, kt=kt, n=n: e.matmul(
                    ob.ap[:, hl * 68:hl * 68 + 65], lhsT=pt.ap[:, hl * 128:(hl + 1) * 128], rhs=V1.ap[:, kt, 0:65],
                    start=(n == 0 and hl == 0), stop=(n == len(kts) - 1), skip_group_check=True),
                    reads=[pt.res, V1.res], writes=[ob.res])
            cur = nxt


    def nsa_setup(self, j):
        sch = self.S
        ms = sch.new_dma_sem("at_misc_sw")
        W1, POST, W2K, W2V, VC1 = self.W1, self.POST, self.W2K, self.W2V, self.VC1
        for kv, (w1, pos, w2) in enumerate(((self.b_k_w1, self.b_k_pos, self.b_k_w2), (self.b_v_w1, self.b_v_pos, self.b_v_w2))):
            p0 = kv * 64
            W1t = W1[kv]
            q0 = 64 - p0
            sch.op("vector", lambda e, W1t=W1t, q0=q0: e.memset(W1t.ap[q0:q0 + 64, :, :], 0.0), writes=[W1t.res])
            sch.op("gpsimd", lambda e, w1=w1, p0=p0, W1t=W1t: e.dma_start(
                out=W1t.ap[p0:p0 + 64, :, :], in_=w1[j].rearrange("(l d) h -> d l h", d=64)), writes=[W1t.res], dma_sem=ms)
            sch.op("gpsimd", lambda e, pos=pos, p0=p0: e.dma_start(
                out=POST.ap[p0:p0 + 64, :], in_=pos[j].rearrange("l d -> d l"), allow_slow_non_contiguous=True),
                writes=[POST.res], dma_sem=ms)
        for dup in range(2):
            sch.op("gpsimd", lambda e, dup=dup: e.dma_start(
                out=W2K.ap[:, :, dup, :], in_=self.b_k_w2[j].rearrange("(c p) d -> p c d", p=128)), writes=[W2K.res], dma_sem=ms)
        sch.op("gpsimd", lambda e: e.dma_start(out=W2V.ap[:], in_=self.b_v_w2[j].rearrange("(c p) d -> p c d", p=128)),
               writes=[W2V.res], dma_sem=ms)
        sch.op("gpsimd", lambda e: e.dma_start(out=VC1.ap[:, :, 65:129], in_=self.ovl.rearrange("c p n -> p c n")),
               writes=[VC1.res], dma_sem=ms)
        sch.op("vector", lambda e: e.memset(VC1.ap[:, :, 64:65], 1.0), writes=[VC1.res])
        ED = self.EXPD
        sch.op("gpsimd", lambda e: e.memset(ED.ap[:], 1.0), writes=[ED.res])
        sch.op("gpsimd", lambda e: e.affine_select(out=ED.ap[:], in_=ED.ap[:], pattern=[[-2, 32], [-1, 2], [0, 64]],
                                                   compare_op=ALU.is_equal, fill=0.0, base=0, channel_multiplier=1),
               [ED.res], [ED.res])
        hb = self.bank[6]
        for kv in range(2):
            p0 = kv * 64
            for hc in range(2):
                col = kv * 2 + hc
                for l in range(32):
                    sch.op("tensor", lambda e, kv=kv, hc=hc, l=l, col=col: e.matmul(
                        hb.ap[:, col:col + 1], lhsT=W1[kv].ap[:, l, hc * 128:(hc + 1) * 128], rhs=POST.ap[:, l:l + 1],
                        start=(l == 0), stop=(l == 31)), reads=[W1[kv].res, POST.res], writes=[hb.res])
        sch.op("vector", lambda e: e.tensor_copy(out=self.HBIAS.ap[:], in_=hb.ap[:, 0:4]), [hb.res], [self.HBIAS.res])

    def compress(self, g):
        sch = self.S
        W1, KVC = self.W1, self.KVC
        for kv in range(2):
            p0 = kv * 64
            hid = self.HID[kv]
            for hc in range(2):
                bk = self.s_ring.next()
                for l in range(32):
                    sch.op("tensor", lambda e, kv=kv, hc=hc, l=l, bk=bk: e.matmul(
                        bk.ap[:, 0:255], lhsT=W1[kv].ap[:, l, hc * 128:(hc + 1) * 128],
                        rhs=KVC.ap[:, l:l + 16 * 254 + 1:16], start=(l == 0), stop=(l == 31)),
                        reads=[W1[kv].res, KVC.res], writes=[bk.res])
                gu = self.GU.next()
                col = kv * 2 + hc
                u, a, b_ = gu.ap[:, 0, 0:255], gu.ap[:, 1, 0:255], gu.ap[:, 2, 0:255]
                sch.op("scalar", lambda e, bk=bk, u=u, col=col: e.activation(out=u, in_=bk.ap[:, 0:255], func=AF.Identity,
                                                                           bias=self.HBIAS.ap[:, col:col + 1], scale=1.0),
                       [bk.res, self.HBIAS.res], [gu.res])
                sch.op("vector", lambda e, u=u, a=a: e.tensor_tensor(out=a, in0=u, in1=u, op=ALU.mult), [gu.res], [gu.res])
                sch.op("vector", lambda e, a=a: e.tensor_scalar(out=a, in0=a, scalar1=0.044715, scalar2=1.0, op0=ALU.mult, op1=ALU.add),
                       [gu.res], [gu.res])
                sch.op("vector", lambda e, u=u, a=a: e.tensor_tensor(out=a, in0=a, in1=u, op=ALU.mult), [gu.res], [gu.res])
                sch.op("scalar", lambda e, a=a, b_=b_: e.activation(out=b_, in_=a, func=AF.Sigmoid, scale=1.5957691216057308),
                       [gu.res], [gu.res])
                sch.op("vector", lambda e, u=u, b_=b_, hid=hid, hc=hc: e.tensor_tensor(out=hid.ap[:, hc, 0:255], in0=u, in1=b_, op=ALU.mult),
                       [gu.res], [hid.res])
        bk = self.s_ring.next()
        hk, hv = self.HID
        for hc in range(2):
            sch.op("tensor", lambda e, hc=hc, bk=bk: e.matmul(
                bk.ap[:, 0:255], lhsT=self.W2K.ap[:, hc].rearrange("p a b -> p (a b)"), rhs=hk.ap[:, hc, 0:255],
                start=(hc == 0), stop=(hc == 1)), reads=[self.W2K.res, hk.res], writes=[bk.res])
        self.copy("vector", self.KCT.ap[:, 0:255], bk.ap[:, 0:255], [bk.res], [self.KCT.res])
        for ct in range(2):
            wc = 128 if ct == 0 else 127
            bk = self.s_ring.next()
            for hc in range(2):
                sch.op("tensor", lambda e, hc=hc, bk=bk, ct=ct, wc=wc: e.matmul(
                    bk.ap[0:wc, 0:64], lhsT=hv.ap[:, hc, ct * 128:ct * 128 + wc], rhs=self.W2V.ap[:, hc, :],
                    start=(hc == 0), stop=(hc == 1)), reads=[hv.res, self.W2V.res], writes=[bk.res])
            self.copy("scalar", self.VC1.ap[0:wc, ct, 0:64], bk.ap[0:wc, 0:64], [bk.res], [self.VC1.res])

    def nsa_q(self, g, i, qt, gl, sz, yo):
        sch = self.S
        oc = self.OC.next()
        imp3 = self.IMP.next()
        imp, impw, impw2 = imp3.ap[:, 0, :], imp3.ap[:, 1, :], imp3.ap[:, 2, :]
        sg = gl
        cts = [0] if i <= 15 else [0, 1]
        first_imp = True
        for half in range(2):
            pts = []
            for ct in cts:
                wc = 128 if ct == 0 else 127
                sbk = self.s_ring.next()
                self.qk(sbk, qt, half, self.KCT.ap[:, ct * 128:ct * 128 + wc], self.KCT.res, width=wc)
                pt = self.PT.next()
                sch.op("scalar", lambda e, pt=pt, sbk=sbk, wc=wc: e.activation(out=pt.ap[0:wc, :], in_=sbk.ap[0:wc, :], func=AF.Exp, scale=0.125),
                       [sbk.res], [pt.res])
                base = 128 * i - 31 - 2048 * ct
                if base - 16 * (wc - 1) < 0:
                    p3 = pt.ap[0:wc, :].rearrange("p (h t) -> p h t", t=128)
                    sch.op("gpsimd", lambda e, p3=p3, base=base: e.affine_select(
                        out=p3, in_=p3, pattern=[[0, 4], [1, 128]], compare_op=ALU.is_ge, fill=0.0, base=base, channel_multiplier=-16),
                        [pt.res], [pt.res])
                pts.append((pt, wc, ct))
            banks = [self.o_cmp.next(), self.o_cmp.next()]
            for hl in range(4):
                ob = banks[hl // 2]
                o0 = (hl % 2) * 132
                for n, (pt, wc, ct) in enumerate(pts):
                    sch.op("tensor", lambda e, ob=ob, o0=o0, pt=pt, wc=wc, ct=ct, n=n, hl=hl, np_=len(pts): e.matmul(
                        ob.ap[:, o0:o0 + 129], lhsT=pt.ap[0:wc, hl * 128:(hl + 1) * 128], rhs=self.VC1.ap[0:wc, ct, 0:129],
                        start=(n == 0 and hl % 2 == 0), stop=(n == np_ - 1), skip_group_check=True),
                        reads=[pt.res, self.VC1.res], writes=[ob.res])
            sm = self.small.next()
            for hl in range(4):
                ob = banks[hl // 2]
                o0 = (hl % 2) * 132
                hh = half * 4 + hl
                sch.op("vector", lambda e, ob=ob, o0=o0, sm=sm, hl=hl: e.tensor_scalar(
                    out=sm.ap[:, hl:hl + 1], in0=ob.ap[:, o0 + 64:o0 + 65], scalar1=1e-30, scalar2=None, op0=ALU.max),
                    [ob.res], [sm.res])
                sch.op("vector", lambda e, sm=sm, hl=hl: e.reciprocal(out=sm.ap[:, hl:hl + 1], in_=sm.ap[:, hl:hl + 1]), [sm.res], [sm.res])
                sch.op("vector", lambda e, ob=ob, o0=o0, sm=sm, hl=hl, hh=hh: e.tensor_scalar(
                    out=oc.ap[:, hh, :], in0=ob.ap[:, o0:o0 + 64], scalar1=sm.ap[:, hl:hl + 1], scalar2=None, op0=ALU.mult),
                    [ob.res, sm.res], [oc.res])
                if first_imp:
                    sch.op("vector", lambda e, ob=ob, o0=o0, sm=sm, hl=hl: e.tensor_scalar(
                        out=imp, in0=ob.ap[:, o0 + 65:o0 + 129], scalar1=sm.ap[:, hl:hl + 1], scalar2=None, op0=ALU.mult),
                        [ob.res, sm.res], [imp3.res])
                    first_imp = False
                else:
                    sch.op("vector", lambda e, ob=ob, o0=o0, sm=sm, hl=hl: e.scalar_tensor_tensor(
                        out=imp, in0=ob.ap[:, o0 + 65:o0 + 129], scalar=sm.ap[:, hl:hl + 1], in1=imp, op0=ALU.mult, op1=ALU.add),
                        [ob.res, sm.res, imp3.res], [imp3.res])
        ir = [imp3.res]
        sch.op("gpsimd", lambda e: e.memset(imp3.ap[0:64, 0, 2 * i + 1:64], -BIG), ir, ir) if 2 * i + 1 < 64 else None
        if 2 * i + 2 < 64:
            sch.op("gpsimd", lambda e: e.memset(imp3.ap[64:128, 0, 2 * i + 2:64], -BIG), ir, ir)
        if i >= 1:
            sch.op("gpsimd", lambda e: e.memset(imp3.ap[0:64, 0, 2 * i - 1:2 * i], 1e30), ir, ir)
        sch.op("gpsimd", lambda e: e.memset(imp3.ap[0:64, 0, 2 * i:2 * i + 1], 2e30), ir, ir)
        sch.op("gpsimd", lambda e: e.memset(imp3.ap[64:128, 0, 2 * i:2 * i + 1], 1e30), ir, ir)
        sch.op("gpsimd", lambda e: e.memset(imp3.ap[64:128, 0, 2 * i + 1:2 * i + 2], 2e30), ir, ir)
        sch.op("gpsimd", lambda e: e.memset(imp3.ap[:, 0, 0:1], 3e30), ir, ir)
        m8 = self.M8.next()
        sch.op("vector", lambda e: e.max(out=m8.ap[:, 0:8], in_=imp), ir, [m8.res])
        sch.op("vector", lambda e: e.match_replace(out=impw, in_to_replace=m8.ap[:, 0:8], in_values=imp, imm_value=-3.0e38),
               [m8.res, imp3.res], ir)
        sch.op("vector", lambda e: e.max(out=m8.ap[:, 8:16], in_=impw), ir, [m8.res])
        sel = self.SEL.next()
        sch.op("vector", lambda e: e.memset(sel.ap[:, 64:128], 0.0), [], [sel.res])
        sch.op("vector", lambda e: e.tensor_scalar(out=sel.ap[:, 0:64], in0=imp, scalar1=m8.ap[:, 15:16], scalar2=None, op0=ALU.is_ge),
               [imp3.res, m8.res], [sel.res])
        selT = self.SELT.next()
        bk = self.tr_ring.next()
        pv = bk.ap[:].bitcast(BF16)
        sch.op("tensor", lambda e: e.transpose(out=pv[:, 0:128], in_=sel.ap[:], identity=self.ident.ap[:]),
               [sel.res, self.ident.res], [bk.res])
        self.copy("vector", selT.ap[:], pv[:, 0:128], [bk.res], [selT.res])
        mk = self.MK.next()
        kt = 0
        while kt <= i:
            n = min(4, i + 1 - kt)
            bk = self.s_ring.next()
            for q in range(n):
                sch.op("tensor", lambda e, bk=bk, q=q, kt=kt: e.matmul(
                    bk.ap[:, q * 128:(q + 1) * 128], lhsT=self.EXPD.ap[0:64, kt + q].rearrange("p a b -> p (a b)"),
                    rhs=selT.ap[0:64, :], start=True, stop=True), reads=[self.EXPD.res, selT.res], writes=[bk.res])
            self.copy(self.evac_eng(), mk.ap[:, kt:kt + n, :], bk.ap[:, 0:n * 128].rearrange("p (n t) -> p n t", t=128),
                      [bk.res], [mk.res])
            kt += n
        sch.op("gpsimd", lambda e: e.affine_select(out=mk.ap[:, i, :], in_=mk.ap[:, i, :], pattern=[[1, 128]],
                                                   compare_op=ALU.is_ge, fill=0.0, base=0, channel_multiplier=-1),
               [mk.res], [mk.res])
        for half in range(2):
            ob_s, ob_w = self.o_slc, self.o_win
            tasks = []
            for kt in range(i + 1):
                tasks.append(dict(KT=self.KT[0], V1=self.V1[0], ob=ob_s, half=half, kt=kt, first=(kt == 0), last=(kt == i),
                                  causal=False, band=False, mk=mk))
            wk = list(range(max(0, i - 4), i + 1))
            for n, kt in enumerate(wk):
                tasks.append(dict(KT=self.KT[1], V1=self.V1[1], ob=ob_w, half=half, kt=kt, first=(n == 0), last=(n == len(wk) - 1),
                                  causal=(kt == i), band=(kt == i - 4), mk=None))
            self.run_tasks(qt, tasks)
            sm = self.small.next()
            osv = ob_s.ap[:, 0:272].rearrange("p (h c) -> p h c", c=68)
            owv = ob_w.ap[:, 0:272].rearrange("p (h c) -> p h c", c=68)
            h4 = half * 4
            sch.op("vector", lambda e, sm=sm, osv=osv: e.reciprocal(out=sm.ap[:, 0:4], in_=osv[:, :, 64]), [ob_s.res], [sm.res])
            sch.op("vector", lambda e, sm=sm, owv=owv: e.reciprocal(out=sm.ap[:, 4:8], in_=owv[:, :, 64]), [ob_w.res], [sm.res])
            sch.op("vector", lambda e, sm=sm, h4=h4: e.tensor_tensor(out=sm.ap[:, 0:4], in0=sm.ap[:, 0:4], in1=sg.ap[:, 8 + h4:12 + h4], op=ALU.mult),
                   [sm.res, sg.res], [sm.res])
            sch.op("vector", lambda e, sm=sm, h4=h4: e.tensor_tensor(out=sm.ap[:, 4:8], in0=sm.ap[:, 4:8], in1=sg.ap[:, 16 + h4:20 + h4], op=ALU.mult),
                   [sm.res, sg.res], [sm.res])
            och = oc.ap[:, h4:h4 + 4, :]
            tmpc = self.TMPC.next()
            sch.op("vector", lambda e, och=och, h4=h4: e.tensor_tensor(
                out=och, in0=och, in1=sg.ap[:, h4:h4 + 4].unsqueeze(2).to_broadcast([128, 4, 64]), op=ALU.mult),
                [oc.res, sg.res], [oc.res])
            for (ov_, c_lo) in ((osv, 0), (owv, 4)):
                obres = ob_s.res if c_lo == 0 else ob_w.res
                sch.op("vector", lambda e, ov_=ov_, c_lo=c_lo, sm=sm, tmpc=tmpc: e.tensor_tensor(
                    out=tmpc.ap[:], in0=ov_[:, :, 0:64], in1=sm.ap[:, c_lo:c_lo + 4].unsqueeze(2).to_broadcast([128, 4, 64]), op=ALU.mult),
                    [obres, sm.res], [tmpc.res])
                sch.op("vector", lambda e, och=och, tmpc=tmpc: e.tensor_tensor(out=och, in0=och, in1=tmpc.ap[:], op=ALU.add),
                       [oc.res, tmpc.res], [oc.res])
            sch.op("vector", lambda e, h4=h4: e.tensor_tensor(
                out=yo.ap[:, h4 * 64:(h4 + 4) * 64], in0=oc.ap[:, h4:h4 + 4, :].rearrange("p h d -> p (h d)"),
                in1=sz.ap[:, h4 * 64:(h4 + 4) * 64], op=ALU.mult), [oc.res, sz.res], [yo.res])

    def build(self):
        self.declare()
        self.setup()
        from_x, from_res = self.x_in, self.xin_res
        for layer in self.layers:
            kind, j = layer % 3, layer // 3
            if kind == 0:
                self.gemm("in", self.a_w_in[j], A_IN, layer, from_x, from_res)
                self.attn_layer("swa", j)
                self.gemm("out", self.a_w_out[j], D, layer, from_x, from_res)
            elif kind == 1:
                self.gemm("in", self.b_w_in[j], B_IN, layer, from_x, from_res)
                self.attn_layer("nsa", j)
                self.gemm("out", self.b_w_out[j], D, layer, from_x, from_res)
            else:
                self.gemm("in", self.c_w_in[j], C_IN, layer, from_x, from_res)
                self.conv_layer(j)
                self.gemm("out", self.c_w_out[j], D, layer, from_x, from_res)
            from_x, from_res = self.xcur, self.xcur_res
        if self.final:
            self.final_norm()
        self.S.emit()
        self.st.close()
        return self.nc


def _consts():
    c = np.zeros((128, 128), np.float32)
    inv = (500000.0 ** (-np.arange(0, 16, 2, dtype=np.float32) / np.float32(16))).astype(np.float32)
    c[:, 0:8] = inv[None, :]
    nc_ = 255
    cs = np.arange(256) * 16
    ce = cs + 32
    ss = np.arange(64) * 64
    se = ss + 64
    ov = np.clip(np.minimum(ce[:, None], se[None, :]) - np.maximum(cs[:, None], ss[None, :]), 0, None) / 32.0
    ov[255] = 0
    return c, ov.reshape(2, 128, 64).astype(np.float32)


_CACHE = {}


def _get_nc(layers=(0, 1, 2, 3), final=True):
    key = (tuple(layers), final)
    if key not in _CACHE:
        _CACHE[key] = Builder(layers, final).build()
    return _CACHE[key]


def make_in_maps(inputs, n_cores=8):
    c, ov = _consts()
    maps = []
    for core in range(n_cores):
        b = core % 4
        pos = np.ascontiguousarray(inputs["positions"][b].reshape(NT, 128).T).astype(np.int32)
        m = {
            "x": np.ascontiguousarray(inputs["x"][b]),
            "posT": pos,
            "norm_w": np.ascontiguousarray(inputs["norm_w"]),
            "final_norm_w": np.ascontiguousarray(inputs["final_norm_w"]).reshape(1, D),
            "consts": c, "ovl": ov,
        }
        for k in ("a_w_in", "a_sinks", "a_w_out", "b_w_in", "b_cmp_k_pos", "b_cmp_k_w1", "b_cmp_k_w2",
                  "b_cmp_v_pos", "b_cmp_v_w1", "b_cmp_v_w2", "b_w_out", "c_w_in", "c_conv_w", "c_w_out"):
            m[k] = np.ascontiguousarray(inputs[k])
        maps.append(m)
    return maps


def kernel(**inputs):
    nc = _get_nc()
    maps = make_in_maps(inputs)
    res = run_bass_kernel_spmd(nc, maps, core_ids=list(range(8)))
    out = np.stack([np.asarray(res.results[b]["out"], dtype=np.float32) for b in range(4)], axis=0)
    return out
```

```python
import contextlib
import numpy as np
import concourse.bass as bass
import concourse.mybir as mybir
from concourse.bass_utils import run_bass_kernel_spmd

F32 = mybir.dt.float32
BF16 = mybir.dt.bfloat16
I32 = mybir.dt.int32
ALU = mybir.AluOpType
AF = mybir.ActivationFunctionType
AX = mybir.AxisListType

SEQ = 4096
D = 2048
NT = SEQ // 128
KC = D // 128
HD = 64
EPS = 1e-5
A_IN = 4608
B_IN = 5728
C_IN = 8192
SC_T = 2048
SC_TILES = SC_T // 128
PI = float(np.pi)
BIG = 1.0e30


class Res:
    __slots__ = ("name", "last_w", "readers")

    def __init__(self, name=""):
        self.name = name
        self.last_w = None
        self.readers = []


class Op:
    __slots__ = ("eng", "fn", "deps", "needs_signal", "sig", "dma_sem", "dma_need")

    def __init__(self, eng, fn, dma_sem):
        self.eng = eng
        self.fn = fn
        self.deps = set()
        self.needs_signal = False
        self.sig = None
        self.dma_sem = dma_sem


class Sched:
    ENGS = ("tensor", "vector", "scalar", "gpsimd", "sync")

    def __init__(self, nc):
        self.nc = nc
        self.ops = []
        self.dma_sems = []
        self.dma_count = {}
        self.bar_deps = set()
        self.bar_dma = {}

    def new_dma_sem(self, name):
        if name not in self.dma_count:
            self.dma_sems.append(name)
            self.dma_count[name] = 0
        return name

    def barrier(self):
        last = {}
        for o in self.ops:
            if o.dma_sem is None:
                last[o.eng] = o
        self.bar_deps = set(last.values())
        self.bar_dma = {k: v for k, v in self.dma_count.items() if v}

    def op(self, eng, fn, reads=(), writes=(), dma_sem=None):
        o = Op(eng, fn, dma_sem)
        for r in reads:
            if r.last_w is not None:
                o.deps.add(r.last_w)
        for w in writes:
            if w.last_w is not None:
                o.deps.add(w.last_w)
            for rd in w.readers:
                o.deps.add(rd)
        for r in reads:
            r.readers.append(o)
        for w in writes:
            w.last_w = o
            w.readers = []
        o.deps.discard(o)
        if eng == "tensor":
            o.deps = {d for d in o.deps if not (d.eng == "tensor" and d.dma_sem is None)}
        o.dma_need = dict(self.bar_dma)
        o.deps |= {d for d in self.bar_deps if not (eng == "tensor" and d.eng == "tensor")}
        for d in o.deps:
            if d.dma_sem is not None:
                o.dma_need[d.dma_sem] = self.dma_count[d.dma_sem]
            else:
                d.needs_signal = True
        if dma_sem is not None:
            self.dma_count[dma_sem] += 16
        self.ops.append(o)
        return o

    def emit(self):
        nc = self.nc
        with contextlib.ExitStack() as st:
            esem = {e: st.enter_context(nc.semaphore("e_" + e)) for e in self.ENGS}
            dsem = {n: st.enter_context(nc.semaphore("d_" + n)) for n in self.dma_sems}
            ecount = {e: 0 for e in self.ENGS}
            dcount = {n: 0 for n in self.dma_sems}
            for o in self.ops:
                if o.dma_sem is not None:
                    dcount[o.dma_sem] += 16
                    o.sig = (("d", o.dma_sem), dcount[o.dma_sem])
                elif o.needs_signal:
                    ecount[o.eng] += 1
                    o.sig = (("e", o.eng), ecount[o.eng])
            per_eng = {e: [] for e in self.ENGS}
            for o in self.ops:
                per_eng[o.eng].append(o)
            block = st.enter_context(nc.Block())

            def make(ename):
                def body(eng):
                    waited = {}
                    for o in per_eng[ename]:
                        need = {}
                        for d in o.deps:
                            if d.dma_sem is not None:
                                continue
                            k, v = d.sig
                            if v > need.get(k, 0):
                                need[k] = v
                        for n, v in o.dma_need.items():
                            need[("d", n)] = v
                        for k, v in need.items():
                            if waited.get(k, 0) >= v:
                                continue
                            waited[k] = v
                            s = esem[k[1]] if k[0] == "e" else dsem[k[1]]
                            eng.wait_ge(s, v)
                        ins = o.fn(eng)
                        if o.sig is not None:
                            k, v = o.sig
                            if k[0] == "d":
                                ins.then_inc(dsem[k[1]], 16)
                            else:
                                ins.then_inc(esem[k[1]], 1)
                    if ename == "sync":
                        for n, c in dcount.items():
                            if c:
                                eng.wait_ge(dsem[n], c)
                        for e, c in ecount.items():
                            if c:
                                eng.wait_ge(esem[e], c)
                return body

            block.tensor(make("tensor"))
            block.vector(make("vector"))
            block.scalar(make("scalar"))
            block.gpsimd(make("gpsimd"))
            block.sync(make("sync"))
        self.stats = dict(ecount=ecount, dcount=dcount, nops=len(self.ops))


class T:
    __slots__ = ("ap", "res")

    def __init__(self, ap, name=""):
        self.ap = ap
        self.res = Res(name)


class Ring:
    def __init__(self, items, sems=None):
        self.items = items
        self.sems = sems
        self.i = -1

    def next(self):
        self.i = (self.i + 1) % len(self.items)
        if self.sems is None:
            return self.items[self.i]
        return self.items[self.i], self.sems[self.i]


class Builder:
    def __init__(self, layers=(0, 1, 2, 3), final=True, dbg=False):
        self.layers = layers
        self.final = final
        self.dbg = dbg
        self.nc = bass.Bass("TRN2", target_bir_lowering=False)
        self.S = Sched(self.nc)
        self.st = contextlib.ExitStack()
        self.alt = 0

    def sb(self, name, shape, dt):
        return T(self.st.enter_context(self.nc.sbuf_tensor(name, shape, dt)), name)

    def dram_in(self, name, shape, dt):
        return self.nc.dram_tensor(name, shape, dt, kind="ExternalInput").ap()

    def ring_sb(self, name, n, shape, dt, sem=True):
        items = [self.sb(f"{name}{i}", shape, dt) for i in range(n)]
        sems = [self.S.new_dma_sem(f"{name}{i}") for i in range(n)] if sem else None
        return Ring(items, sems)

    def evac_eng(self):
        self.alt ^= 1
        return "scalar" if self.alt else "vector"

    def copy(self, eng, out_ap, in_ap, reads, writes):
        if eng == "scalar":
            self.S.op("scalar", lambda e: e.activation(out=out_ap, in_=in_ap, func=AF.Copy), reads, writes)
        else:
            self.S.op(eng, lambda e: e.tensor_copy(out=out_ap, in_=in_ap), reads, writes)

    def declare(self):
        nc, S = self.nc, self.S
        di = self.dram_in
        self.x_in = di("x", [SEQ, D], F32)
        self.posT = di("posT", [128, NT], I32)
        self.norm_w = di("norm_w", [4, D], F32)
        self.final_norm_w = di("final_norm_w", [1, D], F32)
        self.a_w_in = di("a_w_in", [2, D, A_IN], F32)
        self.a_sinks = di("a_sinks", [2, 32], F32)
        self.a_w_out = di("a_w_out", [2, D, D], F32)
        self.b_w_in = di("b_w_in", [1, D, B_IN], F32)
        self.b_k_pos = di("b_cmp_k_pos", [1, 32, 64], F32)
        self.b_k_w1 = di("b_cmp_k_w1", [1, 2048, 256], F32)
        self.b_k_w2 = di("b_cmp_k_w2", [1, 256, 64], F32)
        self.b_v_pos = di("b_cmp_v_pos", [1, 32, 64], F32)
        self.b_v_w1 = di("b_cmp_v_w1", [1, 2048, 256], F32)
        self.b_v_w2 = di("b_cmp_v_w2", [1, 256, 64], F32)
        self.b_w_out = di("b_w_out", [1, D, D], F32)
        self.c_w_in = di("c_w_in", [1, D, C_IN], F32)
        self.c_conv_w = di("c_conv_w", [1, 3, D], F32)
        self.c_w_out = di("c_w_out", [1, D, D], F32)
        self.consts = di("consts", [128, 128], F32)
        self.ovl = di("ovl", [2, 128, 64], F32)
        self.out = nc.dram_tensor("out", [SEQ, D], F32, kind="ExternalOutput").ap()
        self.xcur = nc.dram_tensor("xcur", [SEQ, D], F32, kind="ExternalOutput" if self.dbg else "Internal").ap()
        self.proj = nc.dram_tensor("proj", [SEQ, C_IN], BF16, kind="ExternalInput" if self.dbg == "attn" else "Internal").ap()
        self.ybuf = nc.dram_tensor("ybuf", [SEQ, D], BF16, kind="ExternalOutput" if self.dbg == "attn" else "Internal").ap()
        self.xin_res = [Res(f"xin{i}") for i in range(NT)]
        self.xcur_res = [Res(f"xcur{i}") for i in range(NT)]
        self.proj_res = [Res(f"proj{i}") for i in range(NT)]
        self.y_res = [Res(f"y{i}") for i in range(NT)]
        self.out_res = [Res(f"out{i}") for i in range(NT)]

        sb = self.sb
        self.ident = sb("ident", [128, 128], BF16)
        self.identf = sb("identf", [128, 128], F32)
        self.cst = sb("cst", [128, 128], F32)
        self.cos = sb("cos", [128, NT, 8], F32)
        self.sin = sb("sin", [128, NT, 8], F32)
        self.epsb = sb("epsb", [128, 1], F32)
        self.ARENA_BYTES = 188 * 1024
        self.arena = sb("arena", [128, self.ARENA_BYTES // 2], BF16)
        self.bank = []
        for i in range(8):
            t = T(self.st.enter_context(nc.psum_tensor(f"bank{i}", [128, 512], F32)), f"bank{i}")
            self.bank.append(t)
        self.acc_ring = Ring([self.bank[0], self.bank[1]])
        self.tr_ring = Ring([self.bank[2], self.bank[3]])

    def stage(self):
        self.S.barrier()
        self.arena_off = 0

    def sems(self, prefix, n):
        return [self.S.new_dma_sem(f"{prefix}{k}") for k in range(n)]

    def gemm_alloc(self):
        self.stage()
        self.tr_ring = Ring([self.bank[2], self.bank[3]])
        cv = self.carve
        self.gw = cv("gw", [128, D], F32)
        self.AT = [cv(f"AT{i}", [128, KC, 128], BF16) for i in range(SC_TILES)]
        self.W = self.carve_ring("W", 2, [128, KC, 512], BF16, self.sems("gW", 2))
        self.XIN = self.carve_ring("XIN", 2, [128, D], F32, self.sems("gX", 2))
        self.YIN = self.carve_ring("YIN", 2, [128, D], BF16, self.sems("gY", 2))
        self.HB = self.carve_ring("HB", 2, [128, D], BF16)
        self.OSF = self.carve_ring("OSF", 2, [128, 512], F32, self.sems("gOF", 2))
        self.OSB = self.carve_ring("OSB", 2, [128, 512], BF16, self.sems("gOB", 2))
        self.XR = self.carve_ring("XR", 2, [128, 512], F32, self.sems("gXR", 2))
        self.ss = self.carve_ring("ss", 4, [128, 2], F32)
        self.FO = self.carve_ring("FO", 2, [128, D], F32, self.sems("gFO", 2))

    def setup(self):
        S = self
        sch = self.S
        cs = sch.new_dma_sem("setup")
        cst, ident, identf = self.cst, self.ident, self.identf
        sch.op("sync", lambda e: e.dma_start(out=cst.ap[:], in_=self.consts[:, :]), writes=[cst.res], dma_sem=cs)
        sch.op("gpsimd", lambda e: e.memset(identf.ap[:], 1.0), writes=[identf.res])
        sch.op("gpsimd", lambda e: e.affine_select(out=identf.ap[:], in_=identf.ap[:], pattern=[[-1, 128]],
                                                   compare_op=ALU.is_equal, fill=0.0, base=0, channel_multiplier=1),
               reads=[identf.res], writes=[identf.res])
        sch.op("vector", lambda e: e.tensor_copy(out=ident.ap[:], in_=identf.ap[:]), reads=[identf.res], writes=[ident.res])
        sch.op("vector", lambda e: e.memset(self.epsb.ap[:], EPS), writes=[self.epsb.res])
        posi = self.sb("posi", [128, NT], I32)
        posf = self.sb("posf", [128, NT], F32)
        ang = self.sb("ang", [128, NT, 8], F32)
        tmp = self.sb("angt", [128, NT, 8], F32)
        ki = self.sb("angk", [128, NT, 8], I32)
        kf = self.sb("angkf", [128, NT, 8], F32)
        sch.op("sync", lambda e: e.dma_start(out=posi.ap[:], in_=self.posT[:, :]), writes=[posi.res], dma_sem=cs)
        sch.op("vector", lambda e: e.tensor_copy(out=posf.ap[:], in_=posi.ap[:]), reads=[posi.res], writes=[posf.res])
        sch.op("vector", lambda e: e.tensor_tensor(out=ang.ap[:], in0=posf.ap[:].unsqueeze(2).to_broadcast([128, NT, 8]),
                                                   in1=cst.ap[:, 0:8].unsqueeze(1).to_broadcast([128, NT, 8]), op=ALU.mult),
               reads=[posf.res, cst.res], writes=[ang.res])
        for which, shift in ((self.sin, 0.0), (self.cos, PI / 2)):
            sch.op("vector", lambda e, shift=shift: e.tensor_scalar(out=tmp.ap[:], in0=ang.ap[:], scalar1=shift, scalar2=None, op0=ALU.add),
                   reads=[ang.res], writes=[tmp.res])
            sch.op("vector", lambda e: e.tensor_scalar(out=ki.ap[:], in0=tmp.ap[:], scalar1=1.0 / (2 * PI), scalar2=None, op0=ALU.mult),
                   reads=[tmp.res], writes=[ki.res])
            sch.op("vector", lambda e: e.tensor_copy(out=kf.ap[:], in_=ki.ap[:]), reads=[ki.res], writes=[kf.res])
            sch.op("vector", lambda e: e.scalar_tensor_tensor(out=tmp.ap[:], in0=kf.ap[:], scalar=-2 * PI, in1=tmp.ap[:],
                                                              op0=ALU.mult, op1=ALU.add), reads=[kf.res, tmp.res], writes=[tmp.res])
            sch.op("vector", lambda e: e.tensor_scalar(out=tmp.ap[:], in0=tmp.ap[:], scalar1=3.14159, scalar2=-3.14159,
                                                       op0=ALU.min, op1=ALU.max), reads=[tmp.res], writes=[tmp.res])
            sch.op("scalar", lambda e, which=which: e.activation(out=which.ap[:], in_=tmp.ap[:], func=AF.Sin),
                   reads=[tmp.res], writes=[which.res])

    def transposes(self, src_ap, src_res, nblk, dst_fn, dst_res):
        sch = self.S
        ident = self.ident
        c = 0
        while c < nblk:
            n = min(4, nblk - c)
            bk = self.tr_ring.next()
            pv = bk.ap[:].bitcast(BF16)
            for j in range(n):
                sch.op("tensor", lambda e, j=j, c=c, pv=pv: e.transpose(out=pv[:, j * 128:(j + 1) * 128],
                                                                      in_=src_ap[:, (c + j) * 128:(c + j + 1) * 128],
                                                                      identity=ident.ap[:]),
                       reads=[src_res, ident.res], writes=[bk.res])
            dst = dst_fn(c, n)
            srcv = pv[:, 0:n * 128].rearrange("p (n t) -> p n t", t=128)
            self.copy(self.evac_eng(), dst, srcv, [bk.res], dst_res)
            c += n

    def gemm(self, kind, w_ap, ncols, layer, src_x=None, src_x_res=None):
        sch = self.S
        nc = self.nc
        self.gemm_alloc()
        wv = w_ap.rearrange("(kc p) n -> p kc n", p=128)
        ncb = (ncols + 511) // 512
        if kind == "in":
            gsem = sch.new_dma_sem(f"gw{layer}")
            gw = self.gw
            sch.op("sync", lambda e: e.dma_start(out=gw.ap[:], in_=self.norm_w[layer:layer + 1, :].to_broadcast([128, D])),
                   writes=[gw.res], dma_sem=gsem)
        for sc in range(NT // SC_TILES):
            for tl in range(SC_TILES):
                i = sc * SC_TILES + tl
                at = self.AT[tl]
                if kind == "in":
                    xt, xsem = self.XIN.next()
                    sch.op("sync", lambda e, xt=xt, i=i: e.dma_start(out=xt.ap[:], in_=src_x[i * 128:(i + 1) * 128, :]),
                           reads=[src_x_res[i]], writes=[xt.res], dma_sem=xsem)
                    ss = self.ss.next()
                    hb = self.HB.next()
                    sch.op("scalar", lambda e, xt=xt, ss=ss, hb=hb: e.activation(out=hb.ap[:], in_=xt.ap[:], func=AF.Square,
                                                                        accum_out=ss.ap[:, 0:1]),
                           reads=[xt.res], writes=[hb.res, ss.res])
                    sch.op("scalar", lambda e, ss=ss: e.activation(out=ss.ap[:, 1:2], in_=ss.ap[:, 0:1], func=AF.Sqrt,
                                                                 bias=self.epsb.ap[:, 0:1], scale=1.0 / D),
                           reads=[ss.res, self.epsb.res], writes=[ss.res])
                    sch.op("vector", lambda e, ss=ss: e.reciprocal(out=ss.ap[:, 1:2], in_=ss.ap[:, 1:2]),
                           reads=[ss.res], writes=[ss.res])
                    sch.op("vector", lambda e, xt=xt, ss=ss, hb=hb: e.scalar_tensor_tensor(
                        out=hb.ap[:], in0=xt.ap[:], scalar=ss.ap[:, 1:2], in1=self.gw.ap[:], op0=ALU.mult, op1=ALU.mult),
                        reads=[xt.res, ss.res, self.gw.res], writes=[hb.res])
                    a_ap, a_res = hb.ap, hb.res
                else:
                    yt, ysem = self.YIN.next()
                    sch.op("sync", lambda e, yt=yt, i=i: e.dma_start(out=yt.ap[:], in_=self.ybuf[i * 128:(i + 1) * 128, :]),
                           reads=[self.y_res[i]], writes=[yt.res], dma_sem=ysem)
                    a_ap, a_res = yt.ap, yt.res
                self.transposes(a_ap, a_res, KC, lambda c, n, at=at: at.ap[:, c:c + n, :], [at.res])
            for cb in range(ncb):
                c0 = cb * 512
                cw = min(512, ncols - c0)
                wt, wsem = self.W.next()
                sch.op("gpsimd", lambda e, wt=wt, c0=c0, cw=cw: e.dma_start(out=wt.ap[:, :, 0:cw], in_=wv[:, :, c0:c0 + cw]),
                       writes=[wt.res], dma_sem=wsem)
                for tl in range(SC_TILES):
                    i = sc * SC_TILES + tl
                    at = self.AT[tl]
                    acc = self.acc_ring.next()
                    for kc in range(KC):
                        sch.op("tensor", lambda e, acc=acc, at=at, wt=wt, kc=kc, cw=cw: e.matmul(
                            acc.ap[:, 0:cw], lhsT=at.ap[:, kc, :], rhs=wt.ap[:, kc, 0:cw], start=(kc == 0), stop=(kc == KC - 1)),
                            reads=[at.res, wt.res], writes=[acc.res])
                    if kind == "in":
                        ob, osem = self.OSB.next()
                        self.copy(self.evac_eng(), ob.ap[:, 0:cw], acc.ap[:, 0:cw], [acc.res], [ob.res])
                        sch.op("sync", lambda e, ob=ob, i=i, c0=c0, cw=cw: e.dma_start(
                            out=self.proj[i * 128:(i + 1) * 128, c0:c0 + cw], in_=ob.ap[:, 0:cw]),
                            reads=[ob.res], writes=[self.proj_res[i]], dma_sem=osem)
                    else:
                        xr, xsem = self.XR.next()
                        sch.op("sync", lambda e, xr=xr, i=i, c0=c0, cw=cw: e.dma_start(
                            out=xr.ap[:, 0:cw], in_=src_x[i * 128:(i + 1) * 128, c0:c0 + cw]),
                            reads=[src_x_res[i]], writes=[xr.res], dma_sem=xsem)
                        of, osem = self.OSF.next()
                        sch.op("vector", lambda e, of=of, acc=acc, xr=xr, cw=cw: e.tensor_tensor(
                            out=of.ap[:, 0:cw], in0=acc.ap[:, 0:cw], in1=xr.ap[:, 0:cw], op=ALU.add),
                            reads=[acc.res, xr.res], writes=[of.res])
                        sch.op("sync", lambda e, of=of, i=i, c0=c0, cw=cw: e.dma_start(
                            out=self.xcur[i * 128:(i + 1) * 128, c0:c0 + cw], in_=of.ap[:, 0:cw]),
                            reads=[of.res], writes=[self.xcur_res[i]], dma_sem=osem)

    def final_norm(self):
        sch = self.S
        self.gemm_alloc()
        gsem = sch.new_dma_sem("gwf")
        gw = self.gw
        sch.op("sync", lambda e: e.dma_start(out=gw.ap[:], in_=self.final_norm_w[0:1, :].to_broadcast([128, D])),
               writes=[gw.res], dma_sem=gsem)
        fo = self.FO
        for i in range(NT):
            xt, xsem = self.XIN.next()
            sch.op("sync", lambda e, xt=xt, i=i: e.dma_start(out=xt.ap[:], in_=self.xcur[i * 128:(i + 1) * 128, :]),
                   reads=[self.xcur_res[i]], writes=[xt.res], dma_sem=xsem)
            ss = self.ss.next()
            ft, fsem = fo.next()
            sch.op("scalar", lambda e, xt=xt, ss=ss, ft=ft: e.activation(out=ft.ap[:], in_=xt.ap[:], func=AF.Square,
                                                                accum_out=ss.ap[:, 0:1]),
                   reads=[xt.res], writes=[ft.res, ss.res])
            sch.op("scalar", lambda e, ss=ss: e.activation(out=ss.ap[:, 1:2], in_=ss.ap[:, 0:1], func=AF.Sqrt,
                                                         bias=self.epsb.ap[:, 0:1], scale=1.0 / D),
                   reads=[ss.res, self.epsb.res], writes=[ss.res])
            sch.op("vector", lambda e, ss=ss: e.reciprocal(out=ss.ap[:, 1:2], in_=ss.ap[:, 1:2]), reads=[ss.res], writes=[ss.res])
            sch.op("vector", lambda e, xt=xt, ss=ss, ft=ft: e.scalar_tensor_tensor(
                out=ft.ap[:], in0=xt.ap[:], scalar=ss.ap[:, 1:2], in1=gw.ap[:], op0=ALU.mult, op1=ALU.mult),
                reads=[xt.res, ss.res, gw.res], writes=[ft.res])
            sch.op("sync", lambda e, ft=ft, i=i: e.dma_start(out=self.out[i * 128:(i + 1) * 128, :], in_=ft.ap[:]),
                   reads=[ft.res], writes=[self.out_res[i]], dma_sem=fsem)


    def carve(self, name, shape, dt, parts=128):
        esz = 4 if dt in (F32, I32) else 2
        n = int(np.prod(shape[1:]))
        nbytes = (n * esz + 31) // 32 * 32
        o2 = self.arena_off // 2
        ap = self.arena.ap[0:parts, o2:o2 + nbytes // 2]
        if esz == 4:
            ap = ap.bitcast(dt)
        ap = ap[:, 0:n]
        if len(shape) == 3:
            ap = ap.rearrange("p (a b) -> p a b", b=shape[2])
        elif len(shape) == 4:
            ap = ap.rearrange("p (a b c) -> p a b c", b=shape[2], c=shape[3])
        self.arena_off += nbytes
        assert self.arena_off <= self.ARENA_BYTES, (name, self.arena_off)
        return T(ap, name)

    def carve_ring(self, name, n, shape, dt, sems=None):
        return Ring([self.carve(f"{name}{i}", shape, dt) for i in range(n)], sems)

    def conv_layer(self, j):
        sch = self.S
        self.stage()
        CW = 1024
        NB = 3
        lsem = [sch.new_dma_sem(f"cvl{k}") for k in range(NB)]
        ssem = [sch.new_dma_sem(f"cvs{k}") for k in range(NB)]
        wsem = sch.new_dma_sem("cvw")
        U = self.carve_ring("cvU", NB, [128, 3, CW], BF16, lsem)
        Cg = self.carve_ring("cvC", NB, [128, 3, CW], BF16, lsem)
        Bg = self.carve_ring("cvB", NB, [128, CW], BF16, lsem)
        Zg = self.carve_ring("cvZ", NB, [128, CW], BF16, lsem)
        V = self.carve_ring("cvV", 2, [128, 3, CW], F32)
        SZ = self.carve_ring("cvSZ", 2, [128, CW], F32)
        YO = self.carve_ring("cvY", NB, [128, CW], BF16, ssem)
        CWT = self.carve("cvW", [128, 3, CW], F32)
        for cc in range(D // CW):
            c0 = cc * CW
            sch.op("sync", lambda e, c0=c0: e.dma_start(
                out=CWT.ap[:], in_=self.c_conv_w[j:j + 1, :, c0:c0 + CW].to_broadcast([128, 3, CW])),
                writes=[CWT.res], dma_sem=wsem)
            for i in range(NT):
                u, ls = U.next()
                c, _ = Cg.next()
                b, _ = Bg.next()
                z, _ = Zg.next()
                v = V.next()
                sz = SZ.next()
                yo, ss_ = YO.next()
                r0 = i * 128
                if i == 0:
                    for s in range(2):
                        sh = 2 - s
                        sch.op("gpsimd", lambda e, u=u, s=s, sh=sh: e.memset(u.ap[0:sh, s, :], 0.0), writes=[u.res])
                        sch.op("gpsimd", lambda e, c=c, s=s, sh=sh: e.memset(c.ap[0:sh, s, :], 0.0), writes=[c.res])
                    for s in range(3):
                        sh = 2 - s
                        sch.op("sync", lambda e, u=u, s=s, sh=sh, c0=c0: e.dma_start(
                            out=u.ap[sh:128, s, :], in_=self.proj[0:128 - sh, c0:c0 + CW]),
                            reads=[self.proj_res[0]], writes=[u.res], dma_sem=ls)
                        sch.op("sync", lambda e, c=c, s=s, sh=sh, c0=c0: e.dma_start(
                            out=c.ap[sh:128, s, :], in_=self.proj[0:128 - sh, 2 * D + c0:2 * D + c0 + CW]),
                            reads=[self.proj_res[0]], writes=[c.res], dma_sem=ls)
                else:
                    for (dst, cb) in ((u, c0), (c, 2 * D + c0)):
                        base = self.proj[r0 - 2:r0 + 126, cb:cb + CW]
                        src = bass.AP(tensor=base.tensor, offset=base.offset,
                                      ap=[list(base.ap[0]), [base.ap[0][0], 3], list(base.ap[1])])
                        sch.op("sync", lambda e, dst=dst, src=src: e.dma_start(out=dst.ap[:], in_=src),
                               reads=[self.proj_res[i], self.proj_res[i - 1]], writes=[dst.res], dma_sem=ls)
                sch.op("scalar", lambda e, b=b, r0=r0, c0=c0: e.dma_start(
                    out=b.ap[:], in_=self.proj[r0:r0 + 128, D + c0:D + c0 + CW]),
                    reads=[self.proj_res[i]], writes=[b.res], dma_sem=ls)
                sch.op("scalar", lambda e, z=z, r0=r0, c0=c0: e.dma_start(
                    out=z.ap[:], in_=self.proj[r0:r0 + 128, 3 * D + c0:3 * D + c0 + CW]),
                    reads=[self.proj_res[i]], writes=[z.res], dma_sem=ls)
                sch.op("vector", lambda e, u=u, c=c, v=v: e.tensor_tensor(out=v.ap[:], in0=u.ap[:], in1=c.ap[:], op=ALU.mult),
                       reads=[u.res, c.res], writes=[v.res])
                sch.op("vector", lambda e, v=v: e.tensor_tensor(out=v.ap[:], in0=v.ap[:], in1=CWT.ap[:], op=ALU.mult),
                       reads=[v.res, CWT.res], writes=[v.res])
                sch.op("vector", lambda e, v=v: e.tensor_tensor(out=v.ap[:, 0, :], in0=v.ap[:, 0, :], in1=v.ap[:, 1, :], op=ALU.add),
                       reads=[v.res], writes=[v.res])
                sch.op("vector", lambda e, v=v: e.tensor_tensor(out=v.ap[:, 0, :], in0=v.ap[:, 0, :], in1=v.ap[:, 2, :], op=ALU.add),
                       reads=[v.res], writes=[v.res])
                sch.op("vector", lambda e, v=v, b=b: e.tensor_tensor(out=v.ap[:, 0, :], in0=v.ap[:, 0, :], in1=b.ap[:], op=ALU.mult),
                       reads=[v.res, b.res], writes=[v.res])
                sch.op("scalar", lambda e, z=z, sz=sz: e.activation(out=sz.ap[:], in_=z.ap[:], func=AF.Silu),
                       reads=[z.res], writes=[sz.res])
                sch.op("vector", lambda e, v=v, sz=sz, yo=yo: e.tensor_tensor(out=yo.ap[:], in0=v.ap[:, 0, :], in1=sz.ap[:], op=ALU.mult),
                       reads=[v.res, sz.res], writes=[yo.res])
                sch.op("sync", lambda e, yo=yo, r0=r0, c0=c0: e.dma_start(out=self.ybuf[r0:r0 + 128, c0:c0 + CW], in_=yo.ap[:]),
                       reads=[yo.res], writes=[self.y_res[i]], dma_sem=ss_)

    def rope(self, eng, x3, res, nh, i):
        sch = self.S
        tmp = self.ropet.next()
        cb = self.cos.ap[:, i, :].unsqueeze(1).to_broadcast([128, nh, 8])
        sb_ = self.sin.ap[:, i, :].unsqueeze(1).to_broadcast([128, nh, 8])
        x1 = x3[:, :, 0:8]
        x2 = x3[:, :, 8:16]
        t = tmp.ap[:, :, 0:nh, :]
        rr = [res, tmp.res, self.cos.res, self.sin.res]
        sch.op(eng, lambda e: e.tensor_tensor(out=t[:, 0], in0=x1, in1=cb, op=ALU.mult), rr, [tmp.res])
        sch.op(eng, lambda e: e.tensor_tensor(out=t[:, 1], in0=x2, in1=sb_, op=ALU.mult), rr, [tmp.res])
        sch.op(eng, lambda e: e.tensor_tensor(out=t[:, 2], in0=x2, in1=cb, op=ALU.mult), rr, [tmp.res])
        sch.op(eng, lambda e: e.tensor_tensor(out=t[:, 3], in0=x1, in1=sb_, op=ALU.mult), rr, [tmp.res])
        sch.op(eng, lambda e: e.tensor_tensor(out=x1, in0=t[:, 0], in1=t[:, 1], op=ALU.subtract), [tmp.res], [res])
        sch.op(eng, lambda e: e.tensor_tensor(out=x2, in0=t[:, 2], in1=t[:, 3], op=ALU.add), [tmp.res], [res])

    def attn_alloc(self, mode):
        sch = self.S
        self.stage()
        cv = self.carve
        self.KT = [cv("KT0", [128, SEQ], BF16), cv("KT1", [128, SEQ], BF16)]
        self.V1 = [cv("V10", [128, NT, 72], BF16), cv("V11", [128, NT, 72], BF16)]
        if not hasattr(self, "att_sems"):
            self.att_sems = {k: [sch.new_dma_sem(f"at_{k}{n}") for n in range(3)] for k in ("kl", "zl", "gl")}
            for k in ("ql", "yo"):
                self.att_sems[k] = [sch.new_dma_sem(f"at_{k}{n}") for n in range(4)]
            self.att_misc = sch.new_dma_sem("at_misc")
        self.KL = self.carve_ring("KL", 3, [128, 6, 64], BF16, self.att_sems["kl"])
        self.KD = self.carve_ring("KD", 3, [128, 3, 2, 64], BF16)
        self.ropet = self.carve_ring("ropet", 4, [128, 4, 8, 8], F32)
        self.QL = self.carve_ring("QL", 4, [128, 8, 64], BF16, self.att_sems["ql"])
        self.ZL = self.carve_ring("ZL", 3, [128, 512], BF16, self.att_sems["zl"])
        self.GL = self.carve_ring("GL", 3, [128, 3, 8], BF16, self.att_sems["gl"])
        self.QT = self.carve_ring("QT", 4, [128, 4, 2, 128], BF16)
        self.PT = self.carve_ring("PT", 8, [128, 512], BF16)
        self.SZ = self.carve_ring("SZ", 3, [128, 512], F32)
        self.YO = self.carve_ring("YO", 4, [128, 512], BF16, self.att_sems["yo"])
        self.small = self.carve_ring("small", 6, [128, 64], F32)
        self.SZall = cv("SZall", [128, NT, 512], BF16)
        self.SGall = cv("SGall", [128, NT, 24], F32)
        self.esink = cv("esink", [128, 32], F32)
        if mode == "nsa":
            self.KVC = cv("KVC", [128, SEQ], BF16)
            self.W1 = [cv("W1k", [128, 32, 256], BF16), cv("W1v", [128, 32, 256], BF16)]
            self.POST = cv("POST", [128, 32], BF16)
            self.HBIAS = cv("HBIAS", [128, 4], F32)
            self.W2K = cv("W2K", [128, 2, 2, 64], BF16)
            self.W2V = cv("W2V", [128, 2, 64], BF16)
            self.HID = [cv("HIDk", [128, 2, 256], BF16), cv("HIDv", [128, 2, 256], BF16)]
            self.GU = self.carve_ring("GU", 2, [128, 3, 256], F32)
            self.KCT = cv("KCT", [128, 256], BF16)
            self.VC1 = cv("VC1", [128, 2, 136], BF16)
            self.EXPD = cv("EXPD", [64, 32, 2, 64], BF16, parts=64)
            self.OC = self.carve_ring("OC", 2, [128, 8, 64], F32)
            self.TMPC = self.carve_ring("TMPC", 2, [128, 4, 64], F32)
            self.IMP = self.carve_ring("IMP", 2, [128, 3, 64], F32)
            self.M8 = self.carve_ring("M8", 2, [128, 16], F32)
            self.SEL = self.carve_ring("SEL", 2, [128, 128], BF16)
            self.SELT = self.carve_ring("SELT", 3, [128, 128], BF16)
            self.MK = self.carve_ring("MK", 2, [128, NT, 128], BF16)
        if mode == "swa":
            self.TMPF = self.carve_ring("TMPF", 4, [128, 4, 64], F32)
        self.s_ring = Ring([self.bank[4], self.bank[5]])
        self.s_ring3 = Ring([self.bank[4], self.bank[5], self.bank[0]])
        self.tr_ring = Ring([self.bank[2]]) if mode == "swa" else Ring([self.bank[2], self.bank[3]])
        self.s_ring4 = Ring([self.bank[4], self.bank[5], self.bank[0], self.bank[1]])
        self.o_slc = self.bank[6]
        self.o_win = self.bank[7]
        self.o_cmp = Ring([self.bank[0], self.bank[1]])

    def attn_layer(self, mode, j):
        sch = self.S
        self.attn_alloc(mode)
        ms = self.att_misc
        for qt in self.QT.items:
            sch.op("gpsimd", lambda e, qt=qt: e.memset(qt.ap[:], 0.0), writes=[qt.res])
        for k in range(2):
            v1 = self.V1[k]
            sch.op("gpsimd", lambda e, v1=v1: e.memset(v1.ap[:, :, 64:65], 1.0), writes=[v1.res])
        if mode == "swa":
            es = self.esink
            sch.op("sync", lambda e: e.dma_start(out=es.ap[:], in_=self.a_sinks[j:j + 1, :].to_broadcast([128, 32])),
                   writes=[es.res], dma_sem=ms)
            sch.op("scalar", lambda e: e.activation(out=es.ap[:], in_=es.ap[:], func=AF.Exp), [es.res], [es.res])
            zbase = 2048 + 512
        else:
            if "nosetup" not in getattr(self, "skip", ()):
                self.nsa_setup(j)
            zbase = 3680
        for g in getattr(self, "test_g", range(4)):
            if "nokprep" not in getattr(self, "skip", ()):
                self.kprep(mode, g)
            if mode == "nsa" and "nocompress" not in getattr(self, "skip", ()):
                self.compress(g)
            tiles = list(getattr(self, "test_i", range(NT)))
            for i in tiles:
                zl, zs = self.ZL.next()
                zsrc = self.proj[i * 128:(i + 1) * 128, zbase + g * 512:zbase + (g + 1) * 512]
                sch.op("sync", lambda e, zl=zl, zsrc=zsrc: e.dma_start(out=zl.ap[:], in_=zsrc),
                       reads=[self.proj_res[i]], writes=[zl.res], dma_sem=zs)
                sch.op("scalar", lambda e, zl=zl, i=i: e.activation(out=self.SZall.ap[:, i, :], in_=zl.ap[:], func=AF.Silu),
                       [zl.res], [self.SZall.res])
            if mode == "nsa":
                for i in tiles:
                    gl, gs = self.GL.next()
                    gsrc = self.proj[i * 128:(i + 1) * 128, 3584:3680].rearrange("p (b h) -> p b h", h=32)[:, :, g * 8:(g + 1) * 8]
                    sch.op("sync", lambda e, gl=gl, gsrc=gsrc: e.dma_start(out=gl.ap[:], in_=gsrc),
                           reads=[self.proj_res[i]], writes=[gl.res], dma_sem=gs)
                    sch.op("scalar", lambda e, gl=gl, i=i: e.activation(
                        out=self.SGall.ap[:, i, :], in_=gl.ap[:].rearrange("p b h -> p (b h)"), func=AF.Sigmoid),
                        [gl.res], [self.SGall.res])
            if mode == "swa":
                pairs = [tiles[k:k + 2] for k in range(0, len(tiles), 2)]
                ctxs = [self.q_prep(mode, g, i, zbase) for i in pairs[0]] if pairs else []
                pend = []
                for n, pr in enumerate(pairs):
                    nxt = [self.q_prep(mode, g, i, zbase) for i in pairs[n + 1]] if n + 1 < len(pairs) else []
                    for p_ in pend:
                        p_()
                    pend = self.swa_pair(g, pr, ctxs)
                    ctxs = nxt
                for p_ in pend:
                    p_()
            else:
                ctx = self.q_prep(mode, g, tiles[0], zbase) if tiles else None
                pend = None
                for n, i in enumerate(tiles):
                    nxt = self.q_prep(mode, g, tiles[n + 1], zbase) if n + 1 < len(tiles) else None
                    if pend is not None:
                        pend()
                    pend = self.q_core(mode, g, i, ctx)
                    ctx = nxt
                if pend is not None:
                    pend()

    def swa_pair(self, g, tiles, ctxs):
        sch = self.S
        obanks = [self.bank[6], self.bank[7], self.bank[1], self.bank[3]]
        per = []
        for ti, (i, ctx) in enumerate(zip(tiles, ctxs)):
            kts = list(range(max(0, i - 1), i + 1))
            lst = []
            for n, kt in enumerate(kts):
                for half in range(2):
                    lst.append(dict(KT=self.KT[0], V1=self.V1[0], ob=obanks[ti * 2 + half], half=half, kt=kt, first=(n == 0),
                                    last=(n == len(kts) - 1), causal=(kt == i), band=(kt == i - 1), mk=None, qt=ctx["qt"]))
            per.append(lst)
        tasks = []
        for k in range(max(len(l) for l in per)):
            for l in per:
                if k < len(l):
                    tasks.append(l[k])
        self.run_tasks(None, tasks, L=2, ring=self.s_ring3)
        stores = []
        for ti, (i, ctx) in enumerate(zip(tiles, ctxs)):
            sz = ctx["sz"]
            yo, ys = self.YO.next()
            r0 = i * 128
            for half in range(2):
                ob = obanks[ti * 2 + half]
                sm = self.small.next()
                ov = ob.ap[:, 0:272].rearrange("p (h c) -> p h c", c=68)
                es = self.esink
                h0 = g * 8 + half * 4
                sch.op("vector", lambda e, sm=sm, ov=ov, h0=h0: e.tensor_tensor(
                    out=sm.ap[:, 0:4], in0=ov[:, :, 64], in1=es.ap[:, h0:h0 + 4], op=ALU.add), [ob.res, es.res], [sm.res])
                sch.op("vector", lambda e, sm=sm: e.reciprocal(out=sm.ap[:, 0:4], in_=sm.ap[:, 0:4]), [sm.res], [sm.res])
                c0 = half * 256
                tmpf = self.TMPF.next()
                sch.op("vector", lambda e, sm=sm, ov=ov, tmpf=tmpf: e.tensor_tensor(
                    out=tmpf.ap[:], in0=ov[:, :, 0:64], in1=sm.ap[:, 0:4].unsqueeze(2).to_broadcast([128, 4, 64]), op=ALU.mult),
                    [ob.res, sm.res], [tmpf.res])
                sch.op("vector", lambda e, tmpf=tmpf, c0=c0, yo=yo, sz=sz: e.tensor_tensor(
                    out=yo.ap[:, c0:c0 + 256], in0=tmpf.ap[:].rearrange("p h d -> p (h d)"), in1=sz.ap[:, c0:c0 + 256], op=ALU.mult),
                    [tmpf.res, sz.res], [yo.res])

            def store(yo=yo, ys=ys, r0=r0, i=i):
                sch.op("sync", lambda e: e.dma_start(out=self.ybuf[r0:r0 + 128, g * 512:(g + 1) * 512], in_=yo.ap[:]),
                       reads=[yo.res], writes=[self.y_res[i]], dma_sem=ys)
            stores.append(store)
        return stores

    def kprep(self, mode, g):
        sch = self.S
        npc = 2 if mode == "swa" else 6
        SK = getattr(self, "skip", ())
        for i in range(getattr(self, "test_nk", NT)):
            kl, ks = self.KL.next()
            kd = self.KD.next()
            r0 = i * 128
            src = self.proj[r0:r0 + 128, 2048:2048 + npc * 256].rearrange("p (m c) -> p m c", c=256)[:, :, g * 64:(g + 1) * 64]
            sch.op("sync", lambda e, kl=kl, src=src: e.dma_start(out=kl.ap[:, 0:npc, :], in_=src),
                   reads=[self.proj_res[i]], writes=[kl.res], dma_sem=ks)
            if mode == "swa":
                ropes, dups, vs = [0], [(0, 0)], [(1, 0)]
            else:
                ropes, dups, vs = [2, 4], [(2, 0), (4, 1)], [(3, 0), (5, 1)]
            for m in ropes:
                if "k_norope" not in SK:
                    self.rope("gpsimd", kl.ap[:, m:m + 1, :], kl.res, 1, i)
            for m, slot in dups:
                sch.op("vector", lambda e, kl=kl, kd=kd, m=m, slot=slot: e.tensor_copy(
                    out=kd.ap[:, slot], in_=kl.ap[:, m:m + 1, :].to_broadcast([128, 2, 64])), [kl.res], [kd.res])
            if mode == "nsa":
                sch.op("vector", lambda e, kl=kl, kd=kd: e.tensor_copy(out=kd.ap[:, 2], in_=kl.ap[:, 0:2, :]), [kl.res], [kd.res])
            for m, k in (vs if "k_nov" not in SK else []):
                v1 = self.V1[k]
                sch.op("gpsimd", lambda e, kl=kl, v1=v1, m=m, i=i: e.tensor_copy(out=v1.ap[:, i, 0:64], in_=kl.ap[:, m, :]),
                       [kl.res], [v1.res])
            nd = 1 if mode == "swa" else 3
            dsts = [self.KT[0]] if mode == "swa" else [self.KT[0], self.KT[1], self.KVC]
            bk = self.tr_ring.next()
            pv = bk.ap[:].bitcast(BF16)
            kdf = kd.ap[:].rearrange("p a b c -> p (a b c)")
            for n in range(nd):
                sch.op("tensor", lambda e, n=n, pv=pv, kdf=kdf: e.transpose(
                    out=pv[:, n * 128:(n + 1) * 128], in_=kdf[:, n * 128:(n + 1) * 128], identity=self.ident.ap[:]),
                    reads=[kd.res, self.ident.res], writes=[bk.res])
            ev = self.evac_eng()
            for n in range(nd):
                dst = dsts[n]
                self.copy(ev, dst.ap[:, r0:r0 + 128], pv[:, n * 128:(n + 1) * 128], [bk.res], [dst.res])

    def q_prep(self, mode, g, i, zbase):
        sch = self.S
        r0 = i * 128
        ql, qs = self.QL.next()
        sch.op("sync", lambda e: e.dma_start(out=ql.ap[:].rearrange("p h d -> p (h d)"),
                                             in_=self.proj[r0:r0 + 128, g * 512:(g + 1) * 512]),
               reads=[self.proj_res[i]], writes=[ql.res], dma_sem=qs)
        self.rope("gpsimd" if (mode == "nsa" or i % 2 == 0) else "vector", ql.ap[:], ql.res, 8, i)
        qt = self.QT.next()
        bkq = self.tr_ring.next()
        pvq = bkq.ap[:].bitcast(BF16)
        qlf = ql.ap[:].rearrange("p h d -> p (h d)")
        for c in range(4):
            sch.op("tensor", lambda e, c=c: e.transpose(out=pvq[:, c * 128:(c + 1) * 128], in_=qlf[:, c * 128:(c + 1) * 128],
                                                       identity=self.ident.ap[:]), [ql.res, self.ident.res], [bkq.res])
        pq3 = pvq[:, 0:512].rearrange("p (n t) -> p n t", t=128)
        self.copy("vector", qt.ap[0:64, :, 0, :], pq3[0:64], [bkq.res], [qt.res])
        self.copy("scalar", qt.ap[64:128, :, 1, :], pq3[64:128], [bkq.res], [qt.res])
        sz = T(self.SZall.ap[:, i, :]); sz.res = self.SZall.res
        sg = None
        if mode == "nsa":
            sg = T(self.SGall.ap[:, i, :]); sg.res = self.SGall.res
        return dict(qt=qt, sz=sz, gl=sg)

    def q_core(self, mode, g, i, ctx):
        sch = self.S
        r0 = i * 128
        qt, sz, gl = ctx["qt"], ctx["sz"], ctx["gl"]
        yo, ys = self.YO.next()
        if mode == "swa":
            obs = [self.o_slc, self.o_win]
            kts = list(range(max(0, i - 1), i + 1))
            tasks = []
            for n, kt in enumerate(kts):
                for half in range(2):
                    tasks.append(dict(KT=self.KT[0], V1=self.V1[0], ob=obs[half], half=half, kt=kt, first=(n == 0),
                                      last=(n == len(kts) - 1), causal=(kt == i), band=(kt == i - 1), mk=None))
            self.run_tasks(qt, tasks)
            for half in range(2):
                ob = obs[half]
                sm = self.small.next()
                ov = ob.ap[:, 0:272].rearrange("p (h c) -> p h c", c=68)
                es = self.esink
                h0 = g * 8 + half * 4
                sch.op("vector", lambda e, sm=sm, ov=ov, h0=h0: e.tensor_tensor(
                    out=sm.ap[:, 0:4], in0=ov[:, :, 64], in1=es.ap[:, h0:h0 + 4], op=ALU.add), [ob.res, es.res], [sm.res])
                sch.op("vector", lambda e, sm=sm: e.reciprocal(out=sm.ap[:, 0:4], in_=sm.ap[:, 0:4]), [sm.res], [sm.res])
                for hl in range(4):
                    c0 = (half * 4 + hl) * 64
                    sch.op("vector", lambda e, sm=sm, ov=ov, hl=hl, c0=c0: e.scalar_tensor_tensor(
                        out=yo.ap[:, c0:c0 + 64], in0=ov[:, hl, 0:64], scalar=sm.ap[:, hl:hl + 1], in1=sz.ap[:, c0:c0 + 64],
                        op0=ALU.mult, op1=ALU.mult), [ob.res, sm.res, sz.res], [yo.res])
        else:
            self.nsa_q(g, i, qt, gl, sz, yo)
        def store():
            sch.op("sync", lambda e: e.dma_start(out=self.ybuf[r0:r0 + 128, g * 512:(g + 1) * 512], in_=yo.ap[:]),
                   reads=[yo.res], writes=[self.y_res[i]], dma_sem=ys)
        return store

    def run_tasks(self, qt, tasks, L=3, ring=None):
        sch = self.S
        banks = {}

        def issue_qk(idx):
            t = tasks[idx]
            bk = (ring or self.s_ring4).next()
            banks[idx] = bk
            kt = t["kt"]
            self.qk(bk, t.get("qt", qt), t["half"], t["KT"].ap[:, kt * 128:(kt + 1) * 128], t["KT"].res)

        for idx in range(min(L, len(tasks))):
            issue_qk(idx)
        for idx, t in enumerate(tasks):
            if idx + L < len(tasks):
                issue_qk(idx + L)
            cur = banks.pop(idx)
            pt = self.PT.next()
            sch.op("scalar", lambda e, pt=pt, cur=cur: e.activation(out=pt.ap[:], in_=cur.ap[:], func=AF.Exp, scale=0.125),
                   [cur.res], [pt.res])
            p3 = pt.ap[:].rearrange("p (h t) -> p h t", t=128)
            if t["mk"] is not None:
                mk, kt = t["mk"], t["kt"]
                eng = "gpsimd" if idx % 8 == 7 else "vector"
                sch.op(eng, lambda e, p3=p3, mk=mk, kt=kt: e.tensor_tensor(
                    out=p3, in0=p3, in1=mk.ap[:, kt:kt + 1, :].to_broadcast([128, 4, 128]), op=ALU.mult),
                    [pt.res, mk.res], [pt.res])
            if t["causal"]:
                sch.op("gpsimd", lambda e, p3=p3: e.affine_select(out=p3, in_=p3, pattern=[[0, 4], [1, 128]],
                                                                  compare_op=ALU.is_ge, fill=0.0, base=0, channel_multiplier=-1),
                       [pt.res], [pt.res])
            if t["band"]:
                sch.op("gpsimd", lambda e, p3=p3: e.affine_select(out=p3, in_=p3, pattern=[[0, 4], [-1, 128]],
                                                                  compare_op=ALU.is_ge, fill=0.0, base=-1, channel_multiplier=1),
                       [pt.res], [pt.res])
            ob, V1, kt = t["ob"], t["V1"], t["kt"]
            for hl in range(4):
                sch.op("tensor", lambda e, hl=hl, pt=pt, kt=kt, ob=ob, V1=V1, first=t["first"], last=t["last"]: e.matmul(
                    ob.ap[:, hl * 68:hl * 68 + 65], lhsT=pt.ap[:, hl * 128:(hl + 1) * 128], rhs=V1.ap[:, kt, 0:65],
                    start=(first and hl == 0), stop=last, skip_group_check=True),
                    reads=[pt.res, V1.res], writes=[ob.res])

    def qk(self, sbk, qt, half, kt_ap, kres, width=128, bias=None):
        sch = self.S
        for hl in range(4):
            hh = half * 4 + hl
            pr, od = hh // 2, hh % 2
            sch.op("tensor", lambda e, hl=hl, pr=pr, od=od: e.matmul(
                sbk.ap[0:width, hl * 128:(hl + 1) * 128], lhsT=kt_ap, rhs=qt.ap[:, pr, od, :],
                start=(hl == 0), stop=(bias is None), skip_group_check=True), reads=[kres, qt.res], writes=[sbk.res])
        if bias is not None:
            bl, br, bres = bias
            sch.op("tensor", lambda e: e.matmul(
                sbk.ap[:, 0:512], lhsT=bl, rhs=br.unsqueeze(1).to_broadcast([64, 4, 128]),
                start=False, stop=True, skip_group_check=True), reads=list(bres), writes=[sbk.res])

    def banded(self, qt, half, i, wt, KT, V1, ob, after_exp=None):
        sch = self.S
        kts = list(range(max(0, i - wt), i + 1))
        sb0 = self.s_ring.next()
        self.qk(sb0, qt, half, KT.ap[:, kts[0] * 128:(kts[0] + 1) * 128], KT.res)
        cur = sb0
        for n, kt in enumerate(kts):
            nxt = None
            if n + 1 < len(kts):
                nxt = self.s_ring.next()
                k2 = kts[n + 1]
                self.qk(nxt, qt, half, KT.ap[:, k2 * 128:(k2 + 1) * 128], KT.res)
            pt = self.PT.next()
            sch.op("scalar", lambda e, pt=pt, cur=cur: e.activation(out=pt.ap[:], in_=cur.ap[:], func=AF.Exp, scale=0.125),
                   [cur.res], [pt.res])
            p3 = pt.ap[:].rearrange("p (h t) -> p h t", t=128)
            SK = getattr(self, "skip", ())
            if kt == i and "mask" not in SK:
                sch.op("gpsimd", lambda e, p3=p3: e.affine_select(out=p3, in_=p3, pattern=[[0, 4], [1, 128]],
                                                                  compare_op=ALU.is_ge, fill=0.0, base=0, channel_multiplier=-1),
                       [pt.res], [pt.res])
            if kt == i - wt and "mask" not in SK:
                sch.op("gpsimd", lambda e, p3=p3: e.affine_select(out=p3, in_=p3, pattern=[[0, 4], [-1, 128]],
                                                                  compare_op=ALU.is_ge, fill=0.0, base=-1, channel_multiplier=1),
                       [pt.res], [pt.res])
            for hl in range(4):
                sch.op("tensor", lambda e, hl=hl, pt=pt, kt=kt, n=n: e.matmul(
                    ob.ap[:, hl * 68:hl * 68 + 65], lhsT=pt.ap[:, hl * 128:(hl + 1) * 128], rhs=V1.ap[:, kt, 0:65],
                    start=(n == 0 and hl == 0), stop=(n == len(kts) - 1), skip_group_check=True),
                    reads=[pt.res, V1.res], writes=[ob.res])
            cur = nxt


    def nsa_setup(self, j):
        sch = self.S
        ms = sch.new_dma_sem("at_misc_sw")
        W1, POST, W2K, W2V, VC1 = self.W1, self.POST, self.W2K, self.W2V, self.VC1
        for kv, (w1, pos, w2) in enumerate(((self.b_k_w1, self.b_k_pos, self.b_k_w2), (self.b_v_w1, self.b_v_pos, self.b_v_w2))):
            p0 = kv * 64
            W1t = W1[kv]
            q0 = 64 - p0
            sch.op("vector", lambda e, W1t=W1t, q0=q0: e.memset(W1t.ap[q0:q0 + 64, :, :], 0.0), writes=[W1t.res])
            sch.op("gpsimd", lambda e, w1=w1, p0=p0, W1t=W1t: e.dma_start(
                out=W1t.ap[p0:p0 + 64, :, :], in_=w1[j].rearrange("(l d) h -> d l h", d=64)), writes=[W1t.res], dma_sem=ms)
            sch.op("gpsimd", lambda e, pos=pos, p0=p0: e.dma_start(
                out=POST.ap[p0:p0 + 64, :], in_=pos[j].rearrange("l d -> d l"), allow_slow_non_contiguous=True),
                writes=[POST.res], dma_sem=ms)
        for dup in range(2):
            sch.op("gpsimd", lambda e, dup=dup: e.dma_start(
                out=W2K.ap[:, :, dup, :], in_=self.b_k_w2[j].rearrange("(c p) d -> p c d", p=128)), writes=[W2K.res], dma_sem=ms)
        sch.op("gpsimd", lambda e: e.dma_start(out=W2V.ap[:], in_=self.b_v_w2[j].rearrange("(c p) d -> p c d", p=128)),
               writes=[W2V.res], dma_sem=ms)
        sch.op("gpsimd", lambda e: e.dma_start(out=VC1.ap[:, :, 65:129], in_=self.ovl.rearrange("c p n -> p c n")),
               writes=[VC1.res], dma_sem=ms)
        sch.op("vector", lambda e: e.memset(VC1.ap[:, :, 64:65], 1.0), writes=[VC1.res])
        ED = self.EXPD
        sch.op("gpsimd", lambda e: e.memset(ED.ap[:], 1.0), writes=[ED.res])
        sch.op("gpsimd", lambda e: e.affine_select(out=ED.ap[:], in_=ED.ap[:], pattern=[[-2, 32], [-1, 2], [0, 64]],
                                                   compare_op=ALU.is_equal, fill=0.0, base=0, channel_multiplier=1),
               [ED.res], [ED.res])
        hb = self.bank[6]
        for kv in range(2):
            p0 = kv * 64
            for hc in range(2):
                col = kv * 2 + hc
                for l in range(32):
                    sch.op("tensor", lambda e, kv=kv, hc=hc, l=l, col=col: e.matmul(
                        hb.ap[:, col:col + 1], lhsT=W1[kv].ap[:, l, hc * 128:(hc + 1) * 128], rhs=POST.ap[:, l:l + 1],
                        start=(l == 0), stop=(l == 31)), reads=[W1[kv].res, POST.res], writes=[hb.res])
        sch.op("vector", lambda e: e.tensor_copy(out=self.HBIAS.ap[:], in_=hb.ap[:, 0:4]), [hb.res], [self.HBIAS.res])

    def compress(self, g):
        sch = self.S
        W1, KVC = self.W1, self.KVC
        for kv in range(2):
            p0 = kv * 64
            hid = self.HID[kv]
            for hc in range(2):
                bk = self.s_ring.next()
                for l in range(32):
                    sch.op("tensor", lambda e, kv=kv, hc=hc, l=l, bk=bk: e.matmul(
                        bk.ap[:, 0:255], lhsT=W1[kv].ap[:, l, hc * 128:(hc + 1) * 128],
                        rhs=KVC.ap[:, l:l + 16 * 254 + 1:16], start=(l == 0), stop=(l == 31)),
                        reads=[W1[kv].res, KVC.res], writes=[bk.res])
                gu = self.GU.next()
                col = kv * 2 + hc
                u, a, b_ = gu.ap[:, 0, 0:255], gu.ap[:, 1, 0:255], gu.ap[:, 2, 0:255]
                sch.op("scalar", lambda e, bk=bk, u=u, col=col: e.activation(out=u, in_=bk.ap[:, 0:255], func=AF.Identity,
                                                                           bias=self.HBIAS.ap[:, col:col + 1], scale=1.0),
                       [bk.res, self.HBIAS.res], [gu.res])
                sch.op("vector", lambda e, u=u, a=a: e.tensor_tensor(out=a, in0=u, in1=u, op=ALU.mult), [gu.res], [gu.res])
                sch.op("vector", lambda e, a=a: e.tensor_scalar(out=a, in0=a, scalar1=0.044715, scalar2=1.0, op0=ALU.mult, op1=ALU.add),
                       [gu.res], [gu.res])
                sch.op("vector", lambda e, u=u, a=a: e.tensor_tensor(out=a, in0=a, in1=u, op=ALU.mult), [gu.res], [gu.res])
                sch.op("scalar", lambda e, a=a, b_=b_: e.activation(out=b_, in_=a, func=AF.Sigmoid, scale=1.5957691216057308),
                       [gu.res], [gu.res])
                sch.op("vector", lambda e, u=u, b_=b_, hid=hid, hc=hc: e.tensor_tensor(out=hid.ap[:, hc, 0:255], in0=u, in1=b_, op=ALU.mult),
                       [gu.res], [hid.res])
        bk = self.s_ring.next()
        hk, hv = self.HID
        for hc in range(2):
            sch.op("tensor", lambda e, hc=hc, bk=bk: e.matmul(
                bk.ap[:, 0:255], lhsT=self.W2K.ap[:, hc].rearrange("p a b -> p (a b)"), rhs=hk.ap[:, hc, 0:255],
                start=(hc == 0), stop=(hc == 1)), reads=[self.W2K.res, hk.res], writes=[bk.res])
        self.copy("vector", self.KCT.ap[:, 0:255], bk.ap[:, 0:255], [bk.res], [self.KCT.res])
        for ct in range(2):
            wc = 128 if ct == 0 else 127
            bk = self.s_ring.next()
            for hc in range(2):
                sch.op("tensor", lambda e, hc=hc, bk=bk, ct=ct, wc=wc: e.matmul(
                    bk.ap[0:wc, 0:64], lhsT=hv.ap[:, hc, ct * 128:ct * 128 + wc], rhs=self.W2V.ap[:, hc, :],
                    start=(hc == 0), stop=(hc == 1)), reads=[hv.res, self.W2V.res], writes=[bk.res])
            self.copy("scalar", self.VC1.ap[0:wc, ct, 0:64], bk.ap[0:wc, 0:64], [bk.res], [self.VC1.res])

    def nsa_q(self, g, i, qt, gl, sz, yo):
        sch = self.S
        oc = self.OC.next()
        imp3 = self.IMP.next()
        imp, impw, impw2 = imp3.ap[:, 0, :], imp3.ap[:, 1, :], imp3.ap[:, 2, :]
        sg = gl
        cts = [0] if i <= 15 else [0, 1]
        first_imp = True
        for half in range(2):
            pts = []
            for ct in cts:
                wc = 128 if ct == 0 else 127
                sbk = self.s_ring.next()
                self.qk(sbk, qt, half, self.KCT.ap[:, ct * 128:ct * 128 + wc], self.KCT.res, width=wc)
                pt = self.PT.next()
                sch.op("scalar", lambda e, pt=pt, sbk=sbk, wc=wc: e.activation(out=pt.ap[0:wc, :], in_=sbk.ap[0:wc, :], func=AF.Exp, scale=0.125),
                       [sbk.res], [pt.res])
                base = 128 * i - 31 - 2048 * ct
                if base - 16 * (wc - 1) < 0:
                    p3 = pt.ap[0:wc, :].rearrange("p (h t) -> p h t", t=128)
                    sch.op("gpsimd", lambda e, p3=p3, base=base: e.affine_select(
                        out=p3, in_=p3, pattern=[[0, 4], [1, 128]], compare_op=ALU.is_ge, fill=0.0, base=base, channel_multiplier=-16),
                        [pt.res], [pt.res])
                pts.append((pt, wc, ct))
            banks = [self.o_cmp.next(), self.o_cmp.next()]
            for hl in range(4):
                ob = banks[hl // 2]
                o0 = (hl % 2) * 132
                for n, (pt, wc, ct) in enumerate(pts):
                    sch.op("tensor", lambda e, ob=ob, o0=o0, pt=pt, wc=wc, ct=ct, n=n, hl=hl, np_=len(pts): e.matmul(
                        ob.ap[:, o0:o0 + 129], lhsT=pt.ap[0:wc, hl * 128:(hl + 1) * 128], rhs=self.VC1.ap[0:wc, ct, 0:129],
                        start=(n == 0 and hl % 2 == 0), stop=(n == np_ - 1), skip_group_check=True),
                        reads=[pt.res, self.VC1.res], writes=[ob.res])
            sm = self.small.next()
            for hl in range(4):
                ob = banks[hl // 2]
                o0 = (hl % 2) * 132
                hh = half * 4 + hl
                sch.op("vector", lambda e, ob=ob, o0=o0, sm=sm, hl=hl: e.tensor_scalar(
                    out=sm.ap[:, hl:hl + 1], in0=ob.ap[:, o0 + 64:o0 + 65], scalar1=1e-30, scalar2=None, op0=ALU.max),
                    [ob.res], [sm.res])
                sch.op("vector", lambda e, sm=sm, hl=hl: e.reciprocal(out=sm.ap[:, hl:hl + 1], in_=sm.ap[:, hl:hl + 1]), [sm.res], [sm.res])
                sch.op("vector", lambda e, ob=ob, o0=o0, sm=sm, hl=hl, hh=hh: e.tensor_scalar(
                    out=oc.ap[:, hh, :], in0=ob.ap[:, o0:o0 + 64], scalar1=sm.ap[:, hl:hl + 1], scalar2=None, op0=ALU.mult),
                    [ob.res, sm.res], [oc.res])
                if first_imp:
                    sch.op("vector", lambda e, ob=ob, o0=o0, sm=sm, hl=hl: e.tensor_scalar(
                        out=imp, in0=ob.ap[:, o0 + 65:o0 + 129], scalar1=sm.ap[:, hl:hl + 1], scalar2=None, op0=ALU.mult),
                        [ob.res, sm.res], [imp3.res])
                    first_imp = False
                else:
                    sch.op("vector", lambda e, ob=ob, o0=o0, sm=sm, hl=hl: e.scalar_tensor_tensor(
                        out=imp, in0=ob.ap[:, o0 + 65:o0 + 129], scalar=sm.ap[:, hl:hl + 1], in1=imp, op0=ALU.mult, op1=ALU.add),
                        [ob.res, sm.res, imp3.res], [imp3.res])
        ir = [imp3.res]
        sch.op("gpsimd", lambda e: e.memset(imp3.ap[0:64, 0, 2 * i + 1:64], -BIG), ir, ir) if 2 * i + 1 < 64 else None
        if 2 * i + 2 < 64:
            sch.op("gpsimd", lambda e: e.memset(imp3.ap[64:128, 0, 2 * i + 2:64], -BIG), ir, ir)
        if i >= 1:
            sch.op("gpsimd", lambda e: e.memset(imp3.ap[0:64, 0, 2 * i - 1:2 * i], 1e30), ir, ir)
        sch.op("gpsimd", lambda e: e.memset(imp3.ap[0:64, 0, 2 * i:2 * i + 1], 2e30), ir, ir)
        sch.op("gpsimd", lambda e: e.memset(imp3.ap[64:128, 0, 2 * i:2 * i + 1], 1e30), ir, ir)
        sch.op("gpsimd", lambda e: e.memset(imp3.ap[64:128, 0, 2 * i + 1:2 * i + 2], 2e30), ir, ir)
        sch.op("gpsimd", lambda e: e.memset(imp3.ap[:, 0, 0:1], 3e30), ir, ir)
        m8 = self.M8.next()
        sch.op("vector", lambda e: e.max(out=m8.ap[:, 0:8], in_=imp), ir, [m8.res])
        sch.op("vector", lambda e: e.match_replace(out=impw, in_to_replace=m8.ap[:, 0:8], in_values=imp, imm_value=-3.0e38),
               [m8.res, imp3.res], ir)
        sch.op("vector", lambda e: e.max(out=m8.ap[:, 8:16], in_=impw), ir, [m8.res])
        sel = self.SEL.next()
        sch.op("vector", lambda e: e.memset(sel.ap[:, 64:128], 0.0), [], [sel.res])
        sch.op("vector", lambda e: e.tensor_scalar(out=sel.ap[:, 0:64], in0=imp, scalar1=m8.ap[:, 15:16], scalar2=None, op0=ALU.is_ge),
               [imp3.res, m8.res], [sel.res])
        selT = self.SELT.next()
        bk = self.tr_ring.next()
        pv = bk.ap[:].bitcast(BF16)
        sch.op("tensor", lambda e: e.transpose(out=pv[:, 0:128], in_=sel.ap[:], identity=self.ident.ap[:]),
               [sel.res, self.ident.res], [bk.res])
        self.copy("vector", selT.ap[:], pv[:, 0:128], [bk.res], [selT.res])
        mk = self.MK.next()
        kt = 0
        while kt <= i:
            n = min(4, i + 1 - kt)
            bk = self.s_ring.next()
            for q in range(n):
                sch.op("tensor", lambda e, bk=bk, q=q, kt=kt: e.matmul(
                    bk.ap[:, q * 128:(q + 1) * 128], lhsT=self.EXPD.ap[0:64, kt + q].rearrange("p a b -> p (a b)"),
                    rhs=selT.ap[0:64, :], start=True, stop=True), reads=[self.EXPD.res, selT.res], writes=[bk.res])
            self.copy(self.evac_eng(), mk.ap[:, kt:kt + n, :], bk.ap[:, 0:n * 128].rearrange("p (n t) -> p n t", t=128),
                      [bk.res], [mk.res])
            kt += n
        sch.op("gpsimd", lambda e: e.affine_select(out=mk.ap[:, i, :], in_=mk.ap[:, i, :], pattern=[[1, 128]],
                                                   compare_op=ALU.is_ge, fill=0.0, base=0, channel_multiplier=-1),
               [mk.res], [mk.res])
        for half in range(2):
            ob_s, ob_w = self.o_slc, self.o_win
            tasks = []
            for kt in range(i + 1):
                tasks.append(dict(KT=self.KT[0], V1=self.V1[0], ob=ob_s, half=half, kt=kt, first=(kt == 0), last=(kt == i),
                                  causal=False, band=False, mk=mk))
            wk = list(range(max(0, i - 4), i + 1))
            for n, kt in enumerate(wk):
                tasks.append(dict(KT=self.KT[1], V1=self.V1[1], ob=ob_w, half=half, kt=kt, first=(n == 0), last=(n == len(wk) - 1),
                                  causal=(kt == i), band=(kt == i - 4), mk=None))
            self.run_tasks(qt, tasks)
            sm = self.small.next()
            osv = ob_s.ap[:, 0:272].rearrange("p (h c) -> p h c", c=68)
            owv = ob_w.ap[:, 0:272].rearrange("p (h c) -> p h c", c=68)
            h4 = half * 4
            sch.op("vector", lambda e, sm=sm, osv=osv: e.reciprocal(out=sm.ap[:, 0:4], in_=osv[:, :, 64]), [ob_s.res], [sm.res])
            sch.op("vector", lambda e, sm=sm, owv=owv: e.reciprocal(out=sm.ap[:, 4:8], in_=owv[:, :, 64]), [ob_w.res], [sm.res])
            sch.op("vector", lambda e, sm=sm, h4=h4: e.tensor_tensor(out=sm.ap[:, 0:4], in0=sm.ap[:, 0:4], in1=sg.ap[:, 8 + h4:12 + h4], op=ALU.mult),
                   [sm.res, sg.res], [sm.res])
            sch.op("vector", lambda e, sm=sm, h4=h4: e.tensor_tensor(out=sm.ap[:, 4:8], in0=sm.ap[:, 4:8], in1=sg.ap[:, 16 + h4:20 + h4], op=ALU.mult),
                   [sm.res, sg.res], [sm.res])
            och = oc.ap[:, h4:h4 + 4, :]
            tmpc = self.TMPC.next()
            sch.op("vector", lambda e, och=och, h4=h4: e.tensor_tensor(
                out=och, in0=och, in1=sg.ap[:, h4:h4 + 4].unsqueeze(2).to_broadcast([128, 4, 64]), op=ALU.mult),
                [oc.res, sg.res], [oc.res])
            for (ov_, c_lo) in ((osv, 0), (owv, 4)):
                obres = ob_s.res if c_lo == 0 else ob_w.res
                sch.op("vector", lambda e, ov_=ov_, c_lo=c_lo, sm=sm, tmpc=tmpc: e.tensor_tensor(
                    out=tmpc.ap[:], in0=ov_[:, :, 0:64], in1=sm.ap[:, c_lo:c_lo + 4].unsqueeze(2).to_broadcast([128, 4, 64]), op=ALU.mult),
                    [obres, sm.res], [tmpc.res])
                sch.op("vector", lambda e, och=och, tmpc=tmpc: e.tensor_tensor(out=och, in0=och, in1=tmpc.ap[:], op=ALU.add),
                       [oc.res, tmpc.res], [oc.res])
            sch.op("vector", lambda e, h4=h4: e.tensor_tensor(
                out=yo.ap[:, h4 * 64:(h4 + 4) * 64], in0=oc.ap[:, h4:h4 + 4, :].rearrange("p h d -> p (h d)"),
                in1=sz.ap[:, h4 * 64:(h4 + 4) * 64], op=ALU.mult), [oc.res, sz.res], [yo.res])

    def build(self):
        self.declare()
        self.setup()
        from_x, from_res = self.x_in, self.xin_res
        for layer in self.layers:
            kind, j = layer % 3, layer // 3
            if kind == 0:
                self.gemm("in", self.a_w_in[j], A_IN, layer, from_x, from_res)
                self.attn_layer("swa", j)
                self.gemm("out", self.a_w_out[j], D, layer, from_x, from_res)
            elif kind == 1:
                self.gemm("in", self.b_w_in[j], B_IN, layer, from_x, from_res)
                self.attn_layer("nsa", j)
                self.gemm("out", self.b_w_out[j], D, layer, from_x, from_res)
            else:
                self.gemm("in", self.c_w_in[j], C_IN, layer, from_x, from_res)
                self.conv_layer(j)
                self.gemm("out", self.c_w_out[j], D, layer, from_x, from_res)
            from_x, from_res = self.xcur, self.xcur_res
        if self.final:
            self.final_norm()
        self.S.emit()
        self.st.close()
        return self.nc


def _consts():
    c = np.zeros((128, 128), np.float32)
    inv = (500000.0 ** (-np.arange(0, 16, 2, dtype=np.float32) / np.float32(16))).astype(np.float32)
    c[:, 0:8] = inv[None, :]
    nc_ = 255
    cs = np.arange(256) * 16
    ce = cs + 32
    ss = np.arange(64) * 64
    se = ss + 64
    ov = np.clip(np.minimum(ce[:, None], se[None, :]) - np.maximum(cs[:, None], ss[None, :]), 0, None) / 32.0
    ov[255] = 0
    return c, ov.reshape(2, 128, 64).astype(np.float32)


_CACHE = {}


def _get_nc(layers=(0, 1, 2, 3), final=True):
    key = (tuple(layers), final)
    if key not in _CACHE:
        _CACHE[key] = Builder(layers, final).build()
    return _CACHE[key]


def make_in_maps(inputs, n_cores=8):
    c, ov = _consts()
    maps = []
    for core in range(n_cores):
        b = core % 4
        pos = np.ascontiguousarray(inputs["positions"][b].reshape(NT, 128).T).astype(np.int32)
        m = {
            "x": np.ascontiguousarray(inputs["x"][b]),
            "posT": pos,
            "norm_w": np.ascontiguousarray(inputs["norm_w"]),
            "final_norm_w": np.ascontiguousarray(inputs["final_norm_w"]).reshape(1, D),
            "consts": c, "ovl": ov,
        }
        for k in ("a_w_in", "a_sinks", "a_w_out", "b_w_in", "b_cmp_k_pos", "b_cmp_k_w1", "b_cmp_k_w2",
                  "b_cmp_v_pos", "b_cmp_v_w1", "b_cmp_v_w2", "b_w_out", "c_w_in", "c_conv_w", "c_w_out"):
            m[k] = np.ascontiguousarray(inputs[k])
        maps.append(m)
    return maps


def kernel(**inputs):
    nc = _get_nc()
    maps = make_in_maps(inputs)
    res = run_bass_kernel_spmd(nc, maps, core_ids=list(range(8)))
    out = np.stack([np.asarray(res.results[b]["out"], dtype=np.float32) for b in range(4)], axis=0)
    return out
```
